# Optimizing a Trainium2 kernel written in Bass

```python
import math
import jax
import jax.numpy as jnp
from jax import lax
import numpy as np

D_MODEL = 1024
BATCH = 2
SEQ = 8192
DEPTH = 2

GRID_W = 64
CTX_LEN = 256
MIX_WIDTH = D_MODEL
EPS = 1e-6

HY_WIDTH = MIX_WIDTH // 4
HY_ORDER = 2
HY_POS_BANDS = 16
HY_EMB = 2 * HY_POS_BANDS + 1
HY_FILT_HIDDEN = 64
HY_DECAY_TARGET = 1e-2
HY_FAST_DECAY_PCT = 0.3
HY_SLOW_DECAY_PCT = 1.5

SG_HEADS = 4
SG_WIDTH = MIX_WIDTH // 4
SG_HEAD_DIM = SG_WIDTH // SG_HEADS
CHUNK = 128

DA_HEADS = 4
DA_WIDTH = MIX_WIDTH // 2
DA_V_DIM = DA_WIDTH // DA_HEADS
DA_HEAD_DIM = DA_V_DIM // 2
Q_BLOCK = 128
ROPE_BASE = 10000.0

HY_IN = (HY_ORDER + 1) * HY_WIDTH
SG_IN = 2 * SG_WIDTH
DA_IN = 3 * DA_WIDTH
IN_WIDTH = HY_IN + SG_IN + DA_IN
FFN_HIDDEN = (-(-8 * D_MODEL // 3) + 255) // 256 * 256

kernel_name = 'hybrid_hyena_gmlp_diffattn_block'


def rmsnorm(x, g):
    xf = x.astype(jnp.float32)
    y = xf * lax.rsqrt(jnp.mean(xf * xf, axis=-1, keepdims=True) + EPS)
    return (y * g.astype(jnp.float32)).astype(x.dtype)


def grid_positions(rows):
    row = jnp.repeat(jnp.arange(rows, dtype=jnp.int32), GRID_W)
    col = jnp.tile(jnp.arange(GRID_W, dtype=jnp.int32), rows)
    return row, col


def _rope_axis(x, pos):
    half = x.shape[-1] // 2
    inv_freq = ROPE_BASE ** (-jnp.arange(half, dtype=jnp.float32) / half)
    ang = pos.astype(jnp.float32)[:, None] * inv_freq[None, :]
    cos = jnp.cos(ang)[:, None, None, :]
    sin = jnp.sin(ang)[:, None, None, :]
    x1, x2 = x[..., :half], x[..., half:]
    return jnp.concatenate([x1 * cos - x2 * sin, x2 * cos + x1 * sin], axis=-1)


def rope_2d(x, row, col):
    r = x.shape[-1] // 2
    xf = x.astype(jnp.float32)
    out = jnp.concatenate([_rope_axis(xf[..., :r], row), _rope_axis(xf[..., r:], col)], axis=-1)
    return out.astype(x.dtype)


def short_conv3(p, w, b):
    pp = jnp.pad(p, ((0, 0), (1, 1), (0, 0)))
    return pp[:, :-2] * w[0] + pp[:, 1:-1] * w[1] + pp[:, 2:] * w[2] + b


def hyena_filters(L, w1, b1, w2, b2, w3, freq):
    t = jnp.linspace(0.0, 1.0, L, dtype=jnp.float32)[:, None]
    bands = jnp.linspace(1e-4, HY_POS_BANDS - 1, HY_POS_BANDS, dtype=jnp.float32)
    ang = (2.0 * math.pi / L) * jnp.arange(L, dtype=jnp.float32)[:, None] * bands[None, :]
    feats = jnp.concatenate([t, jnp.cos(ang), -jnp.sin(ang)], axis=-1)
    h = jnp.sin(freq[0] * (feats @ w1 + b1))
    h = jnp.sin(freq[1] * (h @ w2 + b2))
    h = (h @ w3).astype(jnp.float32).reshape(L, HY_ORDER, 2, HY_WIDTH)
    min_decay = math.log(HY_DECAY_TARGET) / HY_SLOW_DECAY_PCT
    max_decay = math.log(HY_DECAY_TARGET) / HY_FAST_DECAY_PCT
    deltas = jnp.abs(jnp.linspace(min_decay, max_decay, HY_WIDTH, dtype=jnp.float32))
    h = h * jnp.exp(-t[:, :, None, None] * deltas)
    zero = jnp.zeros((1, HY_ORDER, HY_WIDTH), jnp.float32)
    return jnp.concatenate([h[:, :, 0], zero, h[:0:-1, :, 1]], axis=0)


def fftconv(u, k, bias):
    L = u.shape[1]
    uf = u.astype(jnp.float32)
    spec = jnp.fft.rfft(uf, n=2 * L, axis=1) * jnp.fft.rfft(k, n=2 * L, axis=0)[None]
    y = jnp.fft.irfft(spec, n=2 * L, axis=1)[:, :L]
    return (y + uf * bias.astype(jnp.float32)).astype(u.dtype)


def hyena(p, conv_w, conv_b, w1, b1, w2, b2, w3, freq, bias):
    L = p.shape[1]
    x1, x2, v = jnp.split(short_conv3(p, conv_w, conv_b), 3, axis=-1)
    k = hyena_filters(L, w1, b1, w2, b2, w3, freq)
    z = x1 * fftconv(v, k[:, 0], bias[0])
    return x2 * fftconv(z, k[:, 1], bias[1])


def sgu(p, norm_g, w_s, b_s):
    B, L, _ = p.shape
    u, v = jnp.split(jax.nn.gelu(p, approximate=False), 2, axis=-1)
    v = rmsnorm(v, norm_g).reshape(B, L // CHUNK, CHUNK, SG_HEADS, SG_HEAD_DIM)
    mixed = jnp.einsum('hpq,bnqhd->bnphd', w_s, v) + b_s.T[None, None, :, :, None]
    return u * mixed.reshape(B, L, SG_WIDTH)


def qk_heads(p, g):
    B, L, _ = p.shape
    return rmsnorm(p.reshape(B, L, DA_HEADS, 2, DA_HEAD_DIM), g)


def v_heads(p):
    B, L, _ = p.shape
    return p.reshape(B, L, DA_HEADS, DA_V_DIM)


def diff_attn_block(q, k, v, lam):
    s = jnp.einsum('bqhcd,bkhcd->bhcqk', q, k).astype(jnp.float32) * (DA_HEAD_DIM ** -0.5)
    a = jax.nn.softmax(s, axis=-1)
    w = a[:, :, 0] - lam * a[:, :, 1]
    return jnp.einsum('bhqk,bkhd->bqhd', w.astype(v.dtype), v)


def diff_attention(q, k, v, lam, subln_g, lam_init):
    B, Lq = q.shape[:2]
    qb = q.reshape(B, Lq // Q_BLOCK, Q_BLOCK, DA_HEADS, 2, DA_HEAD_DIM).swapaxes(0, 1)
    o = lax.map(lambda qq: diff_attn_block(qq, k, v, lam), qb)
    o = o.swapaxes(0, 1).reshape(B, Lq, DA_HEADS, DA_V_DIM)
    o = rmsnorm(o, subln_g) * (1.0 - lam_init)
    return o.reshape(B, Lq, DA_WIDTH)


def mix_out(p, att, hy_params, sg_params, w_out):
    hy = hyena(p[..., :HY_IN], *hy_params)
    sg = sgu(p[..., HY_IN:HY_IN + SG_IN], *sg_params)
    return jnp.concatenate([hy, sg, att], axis=-1) @ w_out


def swiglu(h, w1, w2):
    g, u = jnp.split(h @ w1, 2, axis=-1)
    return (jax.nn.silu(g) * u) @ w2


def adaln(cvec, w, b):
    return jnp.split(jax.nn.silu(cvec) @ w + b, 6, axis=-1)


def setup_inputs(seed: int = 0) -> dict:
    key = jax.random.key(seed)
    ks = iter(jax.random.split(key, 32))

    def nrm(shape, scale):
        return jax.random.normal(next(ks), shape, jnp.float32) * scale

    return {
        'x': nrm((BATCH, SEQ, D_MODEL), 1.0),
        'c': nrm((BATCH, D_MODEL), 1.0),
        'ctx': nrm((BATCH, CTX_LEN, D_MODEL), 1.0),
        'c_ctx': nrm((D_MODEL,), 1.0),
        'norm1_g': 1.0 + nrm((DEPTH, D_MODEL), 0.02),
        'norm2_g': 1.0 + nrm((DEPTH, D_MODEL), 0.02),
        'ada_w': nrm((DEPTH, D_MODEL, 6 * D_MODEL), 0.01),
        'ada_b': nrm((DEPTH, 6 * D_MODEL), 0.01),
        'w_in': nrm((DEPTH, D_MODEL, IN_WIDTH), D_MODEL ** -0.5),
        'hy_conv_w': nrm((DEPTH, 3, HY_IN), 3 ** -0.5),
        'hy_conv_b': nrm((DEPTH, HY_IN), 0.01),
        'hy_w1': nrm((DEPTH, HY_EMB, HY_FILT_HIDDEN), HY_EMB ** -0.5),
        'hy_b1': nrm((DEPTH, HY_FILT_HIDDEN), 0.1),
        'hy_w2': nrm((DEPTH, HY_FILT_HIDDEN, HY_FILT_HIDDEN), HY_FILT_HIDDEN ** -0.5),
        'hy_b2': nrm((DEPTH, HY_FILT_HIDDEN), 0.1),
        'hy_w3': nrm((DEPTH, HY_FILT_HIDDEN, HY_ORDER * 2 * HY_WIDTH), 0.02 * HY_FILT_HIDDEN ** -0.5),
        'hy_freq': 1.0 + nrm((DEPTH, 2, HY_FILT_HIDDEN), 0.02),
        'hy_bias': nrm((DEPTH, HY_ORDER, HY_WIDTH), 1.0),
        'sg_norm_g': 1.0 + nrm((DEPTH, SG_WIDTH), 0.02),
        'sg_w': nrm((DEPTH, SG_HEADS, CHUNK, CHUNK), CHUNK ** -0.5),
        'sg_b': 1.0 + nrm((DEPTH, SG_HEADS, CHUNK), 0.1),
        'qn_g': 1.0 + nrm((DEPTH, DA_HEAD_DIM), 0.02),
        'kn_g': 1.0 + nrm((DEPTH, DA_HEAD_DIM), 0.02),
        'lam_p': nrm((DEPTH, 4, DA_HEAD_DIM), 0.1),
        'subln_g': 1.0 + nrm((DEPTH, DA_V_DIM), 0.02),
        'w_out': nrm((DEPTH, MIX_WIDTH, D_MODEL), MIX_WIDTH ** -0.5),
        'ffn_w1': nrm((DEPTH, D_MODEL, 2 * FFN_HIDDEN), D_MODEL ** -0.5),
        'ffn_w2': nrm((DEPTH, FFN_HIDDEN, D_MODEL), FFN_HIDDEN ** -0.5),
    }


def reference(x, c, ctx, c_ctx, norm1_g, norm2_g, ada_w, ada_b, w_in, hy_conv_w, hy_conv_b,
              hy_w1, hy_b1, hy_w2, hy_b2, hy_w3, hy_freq, hy_bias, sg_norm_g, sg_w, sg_b,
              qn_g, kn_g, lam_p, subln_g, w_out, ffn_w1, ffn_w2):
    rows = x.shape[1] // GRID_W
    row, col = grid_positions(rows)
    da_off = HY_IN + SG_IN
    for i in range(DEPTH):
        last = i == DEPTH - 1
        lam_init = 0.8 - 0.6 * math.exp(-0.3 * i)
        sh1_l, sc1_l, g1_l, sh2_l, sc2_l, g2_l = [m[:, None, :] for m in adaln(c, ada_w[i], ada_b[i])]
        sh1_c, sc1_c, g1_c, sh2_c, sc2_c, g2_c = adaln(c_ctx, ada_w[i], ada_b[i])
        hy_params = (hy_conv_w[i], hy_conv_b[i], hy_w1[i], hy_b1[i], hy_w2[i], hy_b2[i],
                     hy_w3[i], hy_freq[i], hy_bias[i])
        sg_params = (sg_norm_g[i], sg_w[i], sg_b[i])
        lp = lam_p[i].astype(jnp.float32)
        lam = jnp.exp(jnp.dot(lp[0], lp[1])) - jnp.exp(jnp.dot(lp[2], lp[3])) + lam_init

        h_l = rmsnorm(x, norm1_g[i]) * (1.0 + sc1_l) + sh1_l
        h_c = rmsnorm(ctx, norm1_g[i]) * (1.0 + sc1_c) + sh1_c
        p_l = h_l @ w_in[i]
        p_c = h_c @ w_in[i]
        a_l = p_l[..., da_off:]
        a_c = p_c[..., da_off:]
        q_l = rope_2d(qk_heads(a_l[..., :DA_WIDTH], qn_g[i]), row, col)
        k_l = rope_2d(qk_heads(a_l[..., DA_WIDTH:2 * DA_WIDTH], kn_g[i]), row, col)
        v_l = v_heads(a_l[..., 2 * DA_WIDTH:])
        k_c = qk_heads(a_c[..., DA_WIDTH:2 * DA_WIDTH], kn_g[i])
        v_c = v_heads(a_c[..., 2 * DA_WIDTH:])
        k_all = jnp.concatenate([k_c, k_l], axis=1)
        v_all = jnp.concatenate([v_c, v_l], axis=1)
        att_l = diff_attention(q_l, k_all, v_all, lam, subln_g[i], lam_init)
        x = x + g1_l * mix_out(p_l, att_l, hy_params, sg_params, w_out[i])
        x = x + g2_l * swiglu(rmsnorm(x, norm2_g[i]) * (1.0 + sc2_l) + sh2_l, ffn_w1[i], ffn_w2[i])

        if not last:
            q_c = qk_heads(a_c[..., :DA_WIDTH], qn_g[i])
            att_c = diff_attention(q_c, k_c, v_c, lam, subln_g[i], lam_init)
            ctx = ctx + g1_c * mix_out(p_c, att_c, hy_params, sg_params, w_out[i])
            ctx = ctx + g2_c * swiglu(rmsnorm(ctx, norm2_g[i]) * (1.0 + sc2_c) + sh2_c, ffn_w1[i], ffn_w2[i])
    return x
```

```python
import math
import numpy as np
import concourse.bass as bass
import concourse.mybir as mybir
from concourse.bass_utils import run_bass_kernel_spmd

F32 = mybir.dt.float32
BF16 = mybir.dt.bfloat16
ALU = mybir.AluOpType
AF = mybir.ActivationFunctionType
AX = mybir.AxisListType

NCORES = 8


class T:
    def __init__(self, h, name):
        self.h = h
        self.name = name
        self.w = None
        self.r = []
        self.sem = None
        self.semcnt = 0

    def __getitem__(self, idx):
        return self.h[idx]


class P:
    ENG = ("pe", "act", "dve", "pool", "sp")

    def __init__(self, nc):
        self.nc = nc
        self.q = {e: [] for e in self.ENG}
        self.cnt = {e: 0 for e in self.ENG}
        self.esem = {e: nc.alloc_semaphore("es_" + e) for e in self.ENG}
        self.waited = {}
        self.nt = 0
        self.out_events = []
        self.reg = {}

    def sb(self, shape, dtype=F32, name=None):
        self.nt += 1
        name = "%s_%d" % (name or "t", self.nt)
        t = T(self.nc.alloc_sbuf_tensor(name, list(shape), dtype), name)
        self.reg[name] = t
        return t

    def ps(self, shape, dtype=F32, name=None):
        self.nt += 1
        name = "%s_%d" % (name or "p", self.nt)
        t = T(self.nc.alloc_psum_tensor(name, list(shape), dtype), name)
        self.reg[name] = t
        return t

    def t(self, ap):
        if ap is None or isinstance(ap, (int, float)):
            return None
        return self.reg.get(ap.name)

    def mm(self, out, lhsT, rhs, start=True, stop=True):
        self.op("pe", lambda e: e.matmul(out, lhsT=lhsT, rhs=rhs, start=start, stop=stop),
                reads=[self.t(lhsT), self.t(rhs)], writes=[self.t(out)])

    def act(self, out, in_, func, bias=0.0, scale=1.0, accum_out=None):
        kw = {}
        if accum_out is not None:
            kw["accum_out"] = accum_out
        self.op("act", lambda e: e.activation(out=out, in_=in_, func=func, bias=bias, scale=scale, **kw),
                reads=[self.t(in_), self.t(bias), self.t(scale)], writes=[self.t(out), self.t(accum_out)])

    def tt(self, eng, out, in0, in1, op):
        self.op(eng, lambda e: e.tensor_tensor(out=out, in0=in0, in1=in1, op=op),
                reads=[self.t(in0), self.t(in1)], writes=[self.t(out)])

    def ts(self, eng, out, in0, s1, s2, op0, op1=None):
        if op1 is None:
            f = lambda e: e.tensor_scalar(out=out, in0=in0, scalar1=s1, scalar2=None, op0=op0)
        else:
            f = lambda e: e.tensor_scalar(out=out, in0=in0, scalar1=s1, scalar2=s2, op0=op0, op1=op1)
        self.op(eng, f, reads=[self.t(in0), self.t(s1), self.t(s2)], writes=[self.t(out)])

    def stt(self, eng, out, in0, scalar, in1, op0, op1):
        self.op(eng, lambda e: e.scalar_tensor_tensor(out=out, in0=in0, scalar=scalar, in1=in1, op0=op0, op1=op1),
                reads=[self.t(in0), self.t(scalar), self.t(in1)], writes=[self.t(out)])

    def copy(self, eng, out, in_):
        if eng == "act":
            f = lambda e: e.copy(out=out, in_=in_)
        else:
            f = lambda e: e.tensor_copy(out=out, in_=in_)
        self.op(eng, f, reads=[self.t(in_)], writes=[self.t(out)])

    def recip(self, out, in_):
        self.op("dve", lambda e: e.reciprocal(out=out, in_=in_), reads=[self.t(in_)], writes=[self.t(out)])

    def reduce(self, out, in_, op=ALU.add, axis=AX.X):
        self.op("dve", lambda e: e.tensor_reduce(out=out, in_=in_, axis=axis, op=op),
                reads=[self.t(in_)], writes=[self.t(out)])

    def memset(self, eng, out, val):
        self.op(eng, lambda e: e.memset(out, val), writes=[self.t(out)])

    def load(self, out, src, q="sp"):
        self.dma(lambda e: e.dma_start(out=out, in_=src), reads=[self.t(src)], writes=[self.t(out)], q=q)

    def store(self, dst, in_, q="pool"):
        self.dma(lambda e: e.dma_start(out=dst, in_=in_), reads=[self.t(in_)], writes=[], q=q, is_output=True)

    def _need(self, eng, ev, waits):
        if ev is None:
            return
        sem, val, src = ev
        if src == "pe" and eng == "pe":
            return
        key = (eng, id(sem))
        if self.waited.get(key, 0) >= val:
            return
        self.waited[key] = val
        waits.append((sem, val))

    def _deps(self, eng, reads, writes):
        waits = []
        for t in reads:
            self._need(eng, t.w, waits)
        for t in writes:
            self._need(eng, t.w, waits)
            for ev in t.r:
                self._need(eng, ev, waits)
        return waits

    def op(self, eng, fn, reads=(), writes=()):
        reads = [t for t in reads if isinstance(t, T)]
        writes = [t for t in writes if isinstance(t, T)]
        waits = self._deps(eng, reads, writes)
        self.cnt[eng] += 1
        ev = (self.esem[eng], self.cnt[eng], eng)
        for t in reads:
            if t not in writes:
                t.r.append(ev)
        for t in writes:
            t.w = ev
            t.r = []
        self.q[eng].append((waits, fn, (self.esem[eng], 1)))

    def dma(self, fn, reads=(), writes=(), q="sp", is_output=False):
        reads = [t for t in reads if isinstance(t, T)]
        writes = [t for t in writes if isinstance(t, T)]
        waits = self._deps(q, reads, writes)
        owner = (writes + reads)[0]
        if owner.sem is None:
            owner.sem = self.nc.alloc_semaphore("ds_" + owner.name)
        owner.semcnt += 16
        ev = (owner.sem, owner.semcnt, "dma")
        for t in reads:
            t.r.append(ev)
        for t in writes:
            t.w = ev
            t.r = []
        if is_output:
            self.out_events.append(ev)
        self.q[q].append((waits, fn, (owner.sem, 16)))

    def finish(self):
        fin = []
        for ev in self.out_events:
            self._need("sp", ev, fin)
        self.q["sp"].append((fin, None, None))
        nc = self.nc

        def replay(eng_name):
            def run(e):
                for waits, fn, inc in self.q[eng_name]:
                    for sem, val in waits:
                        e.wait_ge(sem, val)
                    if fn is not None:
                        ins = fn(e)
                        ins.then_inc(inc[0], inc[1])
            return run

        with nc.Block() as block:
            block.tensor(replay("pe"))
            block.scalar(replay("act"))
            block.vector(replay("dve"))
            block.gpsimd(replay("pool"))
            block.sync(replay("sp"))


D = 1024
B = 2
L = 8192
LC = 256
DEPTH = 2
GRID_W = 64
EPS = 1e-6
HYW = 256
HY_IN = 768
SG_IN = 512
DA_W = 512
IN_W = 2816
FFH = 2816
TPC = 2048
NTOK = TPC + 128
MAGIC = 12582912.0
TWO_PI = 2.0 * math.pi


class Rot:
    def __init__(self, tiles):
        self.tiles = tiles
        self.i = 0

    def next(self):
        t = self.tiles[self.i % len(self.tiles)]
        self.i += 1
        return t


def dram_in(nc, name, shape, dtype=F32):
    return nc.dram_tensor(name, list(shape), dtype, kind="ExternalInput").ap()


def dram_out(nc, name, shape, dtype=F32):
    return nc.dram_tensor(name, list(shape), dtype, kind="ExternalOutput").ap()


def sin_reduced(p, out, arg, tmp):
    p.ts("dve", tmp, arg, 1.0 / TWO_PI, MAGIC, ALU.mult, ALU.add)
    p.ts("dve", tmp, tmp, MAGIC, None, ALU.subtract)
    p.stt("dve", tmp, arg, 1.0 / TWO_PI, tmp, ALU.mult, ALU.subtract)
    p.act(out, tmp, AF.Sin, scale=TWO_PI)


def rstd_from_ss(p, out, ss, inv_n, eps_tile):
    p.act(out, ss, AF.Sqrt, scale=inv_n, bias=eps_tile)
    p.recip(out, out)


PREP_NL = L // NCORES


def build_prep():
    nc = bass.Bass("TRN2", target_bir_lowering=False)
    c3T = dram_in(nc, "c3T", [128, 8, 3])
    adaw = dram_in(nc, "adaw", [DEPTH, 128, 8, 768])
    adab = dram_in(nc, "adab", [DEPTH, 1, 768])
    mod_o = dram_out(nc, "mod", [DEPTH, 3, 768])
    jobs = []
    for li in range(DEPTH):
        jobs.append((li, PREP_NL, "L"))
    jobs.append((0, LC, "C"))
    feats = {"L": dram_in(nc, "featsL", [33, PREP_NL]), "C": dram_in(nc, "featsC", [33, LC])}
    decay = {"L": dram_in(nc, "decayL", [PREP_NL, HYW]), "C": dram_in(nc, "decayC", [LC, HYW])}
    w1 = dram_in(nc, "hw1", [DEPTH, 33, 64])
    w2 = dram_in(nc, "hw2", [DEPTH, 64, 64])
    w3 = dram_in(nc, "hw3", [DEPTH, 64, 1024])
    hb = dram_in(nc, "hb", [DEPTH, 64, 4])
    filt_o = {(li, kind): dram_out(nc, "filt_%d_%s" % (li, kind), [n, 1024]) for li, n, kind in jobs}

    p = P(nc)
    ct = p.sb([128, 8, 3]); p.load(ct[:], c3T)
    st = p.sb([128, 8, 3])
    p.act(st[:], ct[:], AF.Silu)
    wrot = Rot([p.sb([128, 8, 384], name="adaw") for _ in range(2)])
    pm = p.ps([128, 512], name="pm")
    for li in range(DEPTH):
        bt = p.sb([3, 768], name="adab")
        p.load(bt[:], adab[li].broadcast_to([3, 768]))
        mt = p.sb([3, 768], name="modo")
        for h in range(2):
            wt = wrot.next()
            p.load(wt[:], adaw[li, :, :, h * 384:(h + 1) * 384])
            for k in range(8):
                p.mm(pm[0:3, 0:384], st[:, k, :], wt[:, k, :], start=(k == 0), stop=(k == 7))
            p.tt("dve", mt[:, h * 384:(h + 1) * 384], pm[0:3, 0:384], bt[:, h * 384:(h + 1) * 384], ALU.add)
        p.store(mod_o[li], mt[:])
    ph = p.ps([128, 512], name="ph")
    po = Rot([p.ps([128, 512], name="po") for _ in range(2)])
    for li, n, kind in jobs:
        w1t = p.sb([33, 64]); p.load(w1t[:], w1[li])
        w2t = p.sb([64, 64]); p.load(w2t[:], w2[li])
        w3t = p.sb([64, 1024]); p.load(w3t[:], w3[li])
        hbt = p.sb([64, 4]); p.load(hbt[:], hb[li])
        ft = p.sb([33, n]); p.load(ft[:], feats[kind])
        h2 = p.sb([64, n], name="h2")
        nb = min(n, 512)
        for j in range(n // nb):
            sl = slice(j * nb, (j + 1) * nb)
            arg = p.sb([64, nb], name="arg"); tmp = p.sb([64, nb], name="tmp"); h1 = p.sb([64, nb], name="h1")
            p.mm(ph[0:64, 0:nb], w1t[:], ft[:, sl])
            p.ts("dve", arg[:], ph[0:64, 0:nb], hbt[:, 0:1], hbt[:, 1:2], ALU.add, ALU.mult)
            sin_reduced(p, h1[:], arg[:], tmp[:])
            p.mm(ph[0:64, 0:nb], w2t[:], h1[:])
            p.ts("dve", arg[:], ph[0:64, 0:nb], hbt[:, 2:3], hbt[:, 3:4], ALU.add, ALU.mult)
            sin_reduced(p, h2[:, sl], arg[:], tmp[:])
        orot = Rot([p.sb([128, 1024], name="fo") for _ in range(2)])
        for j in range(n // 128):
            dt = p.sb([128, HYW], name="dec")
            p.load(dt[:], decay[kind][j * 128:(j + 1) * 128, :])
            ot = orot.next()
            for h in range(2):
                pt = po.next()
                p.mm(pt[:], h2[:, j * 128:(j + 1) * 128], w3t[:, h * 512:(h + 1) * 512])
                p.tt("dve", ot[:, h * 512:(h + 1) * 512].rearrange("p (g c) -> p g c", c=HYW),
                     pt[:].rearrange("p (g c) -> p g c", c=HYW),
                     dt[:].unsqueeze(1).broadcast_to([128, 2, HYW]), ALU.mult)
            p.store(filt_o[(li, kind)][j * 128:(j + 1) * 128, :], ot[:])
    p.finish()
    return nc


def hyena_consts(n):
    t = np.linspace(0.0, 1.0, n, dtype=np.float32).astype(np.float64)[:, None]
    bands = np.linspace(1e-4, 15, 16, dtype=np.float32).astype(np.float64)
    ang = (2.0 * math.pi / n) * np.arange(n, dtype=np.float64)[:, None] * bands[None, :]
    feats = np.concatenate([t, np.cos(ang), -np.sin(ang)], axis=-1)
    min_decay = math.log(1e-2) / 1.5
    max_decay = math.log(1e-2) / 0.3
    deltas = np.abs(np.linspace(min_decay, max_decay, HYW, dtype=np.float32).astype(np.float64))
    dec = np.exp(-t * deltas[None, :])
    return np.ascontiguousarray(feats.T.astype(np.float32)), dec.astype(np.float32)


def run_prep(inp):
    nc = build_prep()
    c3 = np.concatenate([inp["c"], inp["c_ctx"][None, :]], axis=0)
    c3T = np.ascontiguousarray(c3.T.reshape(8, 128, 3).transpose(1, 0, 2))
    featsL, decL = hyena_consts(L)
    featsC, decC = hyena_consts(LC)
    hb = np.stack([inp["hy_b1"], inp["hy_freq"][:, 0], inp["hy_b2"], inp["hy_freq"][:, 1]], axis=-1)
    maps = []
    for j in range(NCORES):
        aw = inp["ada_w"][:, :, j * 768:(j + 1) * 768].reshape(DEPTH, 8, 128, 768).transpose(0, 2, 1, 3)
        maps.append({
            "c3T": c3T, "adaw": np.ascontiguousarray(aw),
            "adab": np.ascontiguousarray(inp["ada_b"][:, None, j * 768:(j + 1) * 768]),
            "featsL": np.ascontiguousarray(featsL[:, j * PREP_NL:(j + 1) * PREP_NL]), "featsC": featsC,
            "decayL": np.ascontiguousarray(decL[j * PREP_NL:(j + 1) * PREP_NL]), "decayC": decC,
            "hw1": inp["hy_w1"], "hw2": inp["hy_w2"], "hw3": inp["hy_w3"], "hb": np.ascontiguousarray(hb),
        })
    res = run_bass_kernel_spmd(nc, maps, core_ids=list(range(NCORES))).results
    mod = np.concatenate([r["mod"] for r in res], axis=-1)
    filtL = np.stack([np.concatenate([r["filt_%d_L" % li] for r in res], axis=0) for li in range(DEPTH)])
    filtC = res[0]["filt_0_C"]
    return mod, filtL, filtC


def load_mods(p, modv, ng, which):
    out = []
    ngt = p.sb([128, 8], name="ng"); p.load(ngt[:], ng)
    for s in range(2):
        mt = p.sb([128, 6, 8], name="modv"); p.load(mt[:], modv[s])
        gm = p.sb([128, 8], name="gm")
        o = 3 * which
        p.ts("dve", gm[:], mt[:, o + 1, :], 1.0, None, ALU.add)
        p.tt("dve", gm[:], gm[:], ngt[:], ALU.mult)
        out.append((gm, mt, o))
    return out


def norm_mod(p, hT, xT, n, gm, mt, o, ones, eps_t, pss, sq_rot, rstd):
    for k in range(8):
        sq = sq_rot.next()
        p.act(sq[:, :n], xT[:, k, :n], AF.Square)
        p.mm(pss[:, :n], ones[:], sq[:, :n], start=(k == 0), stop=(k == 7))
    rstd_from_ss(p, rstd[:, :n], pss[:, :n], 1.0 / D, eps_t[:])
    for k in range(8):
        p.stt("dve", hT[:, k, :n], xT[:, k, :n], gm[:, k:k + 1], rstd[:, :n], ALU.mult, ALU.mult)
        p.act(hT[:, k, :n], hT[:, k, :n], AF.Identity, bias=mt[:, o, k:k + 1])


A_COLS = [(0, 512), (512, 768), (768, 1280), (1280, 1792), (1792, 2304), (2304, 2816)]


def build_A():
    nc = bass.Bass("TRN2", target_bir_lowering=False)
    xTd = dram_in(nc, "xT", [128, 8, NTOK])
    modv = dram_in(nc, "modv", [2, 128, 6, 8])
    ng = dram_in(nc, "ng", [128, 8])
    wind = dram_in(nc, "win", [128, 8, IN_W])
    sgg = dram_in(nc, "sgg", [1, 256])
    wsT = dram_in(nc, "wsT", [128, 4, 128])
    sgb = dram_in(nc, "sgb", [128, 4])
    qkg = dram_in(nc, "qkg", [2, 1, 64])
    ropeC = dram_in(nc, "ropeC", [128, 16, 64])
    ropeS = dram_in(nc, "ropeS", [128, 16, 64])
    hy_o = dram_out(nc, "p_hy", [NTOK, HY_IN])
    sg_o = dram_out(nc, "sg", [NTOK, 256])
    q_o = dram_out(nc, "q", [NTOK, 512], BF16)
    k_o = dram_out(nc, "k", [NTOK, 512], BF16)
    v_o = dram_out(nc, "v", [NTOK, 512], BF16)

    p = P(nc)
    ones = p.sb([128, 128]); p.memset("dve", ones[:], 1.0)
    eps_t = p.sb([128, 1]); p.memset("dve", eps_t[:], EPS)
    win = p.sb([128, 8, IN_W], name="win"); p.load(win[:], wind)
    mods = load_mods(p, modv, ng, 0)
    sggt = p.sb([128, 256]); p.load(sggt[:], sgg.broadcast_to([128, 256]))
    wst = p.sb([128, 4, 128]); p.load(wst[:], wsT)
    sgbt = p.sb([128, 4]); p.load(sgbt[:], sgb)
    gq = p.sb([128, 64]); p.load(gq[:], qkg[0].broadcast_to([128, 64]))
    p.ts("dve", gq[:], gq[:], 0.125, None, ALU.mult)
    gk = p.sb([128, 64]); p.load(gk[:], qkg[1].broadcast_to([128, 64]))
    rc = p.sb([128, 16, 64]); p.load(rc[:], ropeC)
    rs = p.sb([128, 16, 64]); p.load(rs[:], ropeS)

    xrot = Rot([p.sb([128, 8, 512], name="xT") for _ in range(2)])
    hrot = Rot([p.sb([128, 8, 512], name="hT") for _ in range(2)])
    sq_rot = Rot([p.sb([128, 512], name="sq") for _ in range(2)])
    rstd = p.sb([128, 512], name="rstd")
    pss = p.ps([128, 512], name="pss")
    pmm = Rot([p.ps([128, 512], name="pmm") for _ in range(4)])
    psg = p.ps([128, 256], name="psg")
    o_hy = Rot([p.sb([128, HY_IN], name="ohy") for _ in range(2)])
    o_sg = Rot([p.sb([128, 256], name="osg") for _ in range(2)])
    gel = Rot([p.sb([128, 512], name="gel") for _ in range(2)])
    vn = Rot([p.sb([128, 256], name="vn") for _ in range(2)])
    junk = p.sb([128, 256], name="junk")
    s1 = Rot([p.sb([128, 1], name="s1") for _ in range(2)])
    qsq = Rot([p.sb([128, 512], name="qsq") for _ in range(2)])
    qn = Rot([p.sb([128, 512], name="qn") for _ in range(2)])
    t1 = Rot([p.sb([128, 512], name="t1") for _ in range(2)])
    t2 = Rot([p.sb([128, 512], name="t2") for _ in range(2)])
    s8 = Rot([p.sb([128, 8], name="s8") for _ in range(2)])
    o_qk = Rot([p.sb([128, 512], BF16, name="oqk") for _ in range(3)])

    groups = [(g * 512, 512, 0) for g in range(4)] + [(TPC, 128, 1)]
    for t0, n, ms in groups:
        gm, mt, o = mods[ms]
        xT = xrot.next(); hT = hrot.next()
        p.load(xT[:, :, :n], xTd[:, :, t0:t0 + n])
        norm_mod(p, hT, xT, n, gm, mt, o, ones, eps_t, pss, sq_rot, rstd)
        for s in range(n // 128):
            tok = slice(t0 + s * 128, t0 + (s + 1) * 128)
            ti = (t0 + s * 128) // 128
            ohy = o_hy.next()
            for ci, (c0, c1) in enumerate(A_COLS):
                w = c1 - c0
                pt = pmm.next()
                for k in range(8):
                    p.mm(pt[:, :w], hT[:, k, s * 128:(s + 1) * 128], win[:, k, c0:c1], start=(k == 0), stop=(k == 7))
                if ci < 2:
                    p.copy("act", ohy[:, c0:c1], pt[:, :w])
                    if ci == 1:
                        p.store(hy_o[tok, :], ohy[:])
                elif ci == 2:
                    ge = gel.next(); v_n = vn.next(); ss = s1.next(); osg = o_sg.next()
                    p.act(ge[:], pt[:], AF.Gelu)
                    p.act(junk[:], ge[:, 256:512], AF.Square, accum_out=ss[:])
                    rstd_from_ss(p, ss[:], ss[:], 1.0 / 256, eps_t[:])
                    p.stt("dve", v_n[:], ge[:, 256:512], ss[:, 0:1], sggt[:], ALU.mult, ALU.mult)
                    for h in range(4):
                        p.mm(psg[:, h * 64:(h + 1) * 64], wst[:, h, :], v_n[:, h * 64:(h + 1) * 64])
                    for h in range(4):
                        p.stt("dve", osg[:, h * 64:(h + 1) * 64], psg[:, h * 64:(h + 1) * 64], sgbt[:, h:h + 1],
                              ge[:, h * 64:(h + 1) * 64], ALU.add, ALU.mult)
                    p.store(sg_o[tok, :], osg[:])
                elif ci in (3, 4):
                    g_t = gq if ci == 3 else gk
                    sq = qsq.next(); q_n = qn.next(); s_8 = s8.next(); oq = o_qk.next()
                    p.act(sq[:], pt[:], AF.Square)
                    p.reduce(s_8[:], sq[:].rearrange("p (g d) -> p g d", d=64))
                    rstd_from_ss(p, s_8[:], s_8[:], 1.0 / 64, eps_t[:])
                    p.tt("dve", q_n[:].rearrange("p (g d) -> p g d", d=64), pt[:].rearrange("p (g d) -> p g d", d=64),
                         s_8[:].unsqueeze(2).broadcast_to([128, 8, 64]), ALU.mult)
                    if ms == 0:
                        p.tt("dve", q_n[:].rearrange("p (g d) -> p g d", d=64), q_n[:].rearrange("p (g d) -> p g d", d=64),
                             g_t[:].unsqueeze(1).broadcast_to([128, 8, 64]), ALU.mult)
                        a = t1.next(); b2 = t2.next()
                        p.tt("dve", a[:].rearrange("p (g d) -> p g d", d=64), q_n[:].rearrange("p (g d) -> p g d", d=64),
                             rc[:, ti, :].unsqueeze(1).broadcast_to([128, 8, 64]), ALU.mult)
                        qv = q_n[:].rearrange("p (g a h d) -> p g a h d", g=8, a=2, h=2)
                        bv = b2[:].rearrange("p (g a h d) -> p g a h d", g=8, a=2, h=2)
                        sv = rs[:, ti, :].rearrange("p (a h d) -> p a h d", a=2, h=2)
                        for hh in range(2):
                            p.tt("pool", bv[:, :, :, hh, :], qv[:, :, :, 1 - hh, :],
                                 sv[:, :, hh, :].unsqueeze(1).broadcast_to([128, 8, 2, 16]), ALU.mult)
                        p.tt("dve", oq[:], a[:], b2[:], ALU.add)
                    else:
                        p.tt("dve", oq[:].rearrange("p (g d) -> p g d", d=64), q_n[:].rearrange("p (g d) -> p g d", d=64),
                             g_t[:].unsqueeze(1).broadcast_to([128, 8, 64]), ALU.mult)
                    p.store((q_o if ci == 3 else k_o)[tok, :], oq[:])
                else:
                    ov = o_qk.next()
                    p.copy("act", ov[:], pt[:])
                    p.store(v_o[tok, :], ov[:])
    p.finish()
    return nc


def rope_tables():
    half = 16
    inv = 10000.0 ** (-np.arange(half, dtype=np.float64) / half)
    t = np.arange(L)
    row = (t // GRID_W).astype(np.float64)[:, None] * inv[None, :]
    col = (t % GRID_W).astype(np.float64)[:, None] * inv[None, :]
    C = np.concatenate([np.cos(row), np.cos(row), np.cos(col), np.cos(col)], axis=1)
    S = np.concatenate([-np.sin(row), np.sin(row), -np.sin(col), np.sin(col)], axis=1)
    return C.astype(np.float32), S.astype(np.float32)


def fm(a):
    n = a.shape[0]
    return np.ascontiguousarray(a.T.reshape(8, 128, n).transpose(1, 0, 2))


def vec8(v):
    return np.ascontiguousarray(v.reshape(8, 128).T)


def mods_for_core(mod_l, j):
    b = j // 4
    out = np.zeros((2, 128, 6, 8), np.float32)
    for s, r in enumerate((b, 2)):
        m = mod_l[r].reshape(6, 8, 128)
        out[s] = m.transpose(2, 0, 1)
    return out


def core_tokens(xT_lat, xT_ctx, j):
    raise NotImplementedError


def run_A(li, inp, mod, x_lat, x_ctx):
    nc = build_A()
    ropeC, ropeS = rope_tables()
    maps = []
    for j in range(NCORES):
        b, r = j // 4, j % 4
        cb, cc = (r // 2, r % 2)
        if j >= 4:
            cb, cc = ((j - 4) // 2, (j - 4) % 2)
        xt = np.concatenate([x_lat[b, r * TPC:(r + 1) * TPC], x_ctx[cb, cc * 128:(cc + 1) * 128]], axis=0)
        tl = slice(r * TPC, (r + 1) * TPC)
        maps.append({
            "xT": fm(xt), "modv": mods_for_core(mod[li], j) if j < 4 or True else None, "ng": vec8(inp["norm1_g"][li]),
            "win": np.ascontiguousarray(inp["w_in"][li].reshape(8, 128, IN_W).transpose(1, 0, 2)),
            "sgg": np.ascontiguousarray(inp["sg_norm_g"][li][None, :]),
            "wsT": np.ascontiguousarray(inp["sg_w"][li].transpose(2, 0, 1)),
            "sgb": np.ascontiguousarray(inp["sg_b"][li].T),
            "qkg": np.ascontiguousarray(np.stack([inp["qn_g"][li], inp["kn_g"][li]])[:, None, :]),
            "ropeC": np.ascontiguousarray(ropeC[tl].reshape(16, 128, 64).transpose(1, 0, 2)),
            "ropeS": np.ascontiguousarray(ropeS[tl].reshape(16, 128, 64).transpose(1, 0, 2)),
        })
    res = run_bass_kernel_spmd(nc, maps, core_ids=list(range(NCORES))).results
    out = {}
    for name, w in (("p_hy", HY_IN), ("sg", 256), ("q", 512), ("k", 512), ("v", 512)):
        lat = np.stack([np.concatenate([res[b * 4 + r][name][:TPC] for r in range(4)], axis=0) for b in range(B)])
        ctx = np.stack([np.concatenate([res[cb * 2 + cc][name][TPC:] for cc in range(2)], axis=0) for cb in range(B)])
        out[name] = lat
        out[name + "_c"] = ctx
    return out


NK = LC + L


def build_AT(li, with_ctx):
    lam_init = 0.8 - 0.6 * math.exp(-0.3 * li)
    nc = bass.Bass("TRN2", target_bir_lowering=False)
    qTd = dram_in(nc, "qT", [128, L], BF16)
    kTd = dram_in(nc, "kT", [128, NK], BF16)
    vd = dram_in(nc, "v", [128, NK // 128, 128], BF16)
    lamp = dram_in(nc, "lamp", [1, 256])
    subg = dram_in(nc, "subg", [128, 1])
    o_o = dram_out(nc, "oT", [128, L])
    if with_ctx:
        qcd = dram_in(nc, "qcT", [128, LC], BF16)
        oc_o = dram_out(nc, "ocT", [128, LC])

    p = P(nc)
    ones = p.sb([128, 128]); p.memset("dve", ones[:], 1.0)
    onesb = p.sb([128, 128], BF16); p.memset("dve", onesb[:], 1.0)
    eps_t = p.sb([128, 1]); p.memset("dve", eps_t[:], EPS)
    qT = p.sb([128, L], BF16, name="qT"); p.load(qT[:], qTd)
    kT = p.sb([128, NK], BF16, name="kT"); p.load(kT[:], kTd)
    vt = p.sb([128, NK // 128, 128], BF16, name="v"); p.load(vt[:], vd)
    lp = p.sb([128, 256]); p.load(lp[:], lamp.broadcast_to([128, 256]))
    pr = p.sb([128, 128]); d2 = p.sb([128, 2]); nlam = p.sb([128, 1]); gsub = p.sb([128, 1])
    lv = lp[:].rearrange("p (a b d) -> p a b d", a=2, b=2)
    p.tt("dve", pr[:].rearrange("p (a d) -> p a d", a=2), lv[:, :, 0, :], lv[:, :, 1, :], ALU.mult)
    p.reduce(d2[:], pr[:].rearrange("p (a d) -> p a d", a=2))
    p.act(d2[:], d2[:], AF.Exp)
    p.tt("dve", nlam[:], d2[:, 1:2], d2[:, 0:1], ALU.subtract)
    p.ts("dve", nlam[:], nlam[:], -lam_init, None, ALU.add)
    p.load(gsub[:], subg)
    p.ts("dve", gsub[:], gsub[:], 1.0 - lam_init, None, ALU.mult)
    if with_ctx:
        qc = p.sb([128, LC], BF16); p.load(qc[:], qcd)

    psS = [Rot([p.ps([128, 512], name="S%d" % c) for _ in range(2)]) for c in range(2)]
    psAV = [p.ps([128, 512], name="AV%d" % c) for c in range(2)]
    psSUM = [p.ps([128, 512], name="SUM%d" % c) for c in range(2)]
    Er = [Rot([p.sb([128, 512], BF16, name="E%d" % c) for _ in range(3)]) for c in range(2)]
    rr = Rot([p.sb([128, 512], name="rr") for _ in range(2)])
    aa = Rot([p.sb([128, 512], name="aa") for _ in range(2)])
    oo = Rot([p.sb([128, 512], name="oo") for _ in range(2)])
    sq = p.sb([128, 512], name="sq"); rstd = p.sb([128, 512], name="rstd")
    ob = Rot([p.sb([128, 512], name="ob") for _ in range(2)])

    def attend(qsrc, q0, n, nkt, dst):
        for kt in range(nkt):
            for c in range(2):
                ps = psS[c].next(); E = Er[c].next()
                pc = slice(c * 64, (c + 1) * 64)
                p.mm(ps[:, :n], kT[pc, kt * 128:(kt + 1) * 128], qsrc[pc, q0:q0 + n])
                p.act(E[:, :n], ps[:, :n], AF.Exp)
                p.mm(psAV[c][:, :n], vt[:, kt, :], E[:, :n], start=(kt == 0), stop=(kt == nkt - 1))
                p.mm(psSUM[c][:, :n], onesb[:], E[:, :n], start=(kt == 0), stop=(kt == nkt - 1))
        a_ = []
        for c in range(2):
            r = rr.next(); a = aa.next()
            p.recip(r[:, :n], psSUM[c][:, :n])
            p.tt("dve", a[:, :n], psAV[c][:, :n], r[:, :n], ALU.mult)
            a_.append(a)
        o = oo.next()
        p.stt("dve", o[:, :n], a_[1][:, :n], nlam[:, 0:1], a_[0][:, :n], ALU.mult, ALU.add)
        p.act(sq[:, :n], o[:, :n], AF.Square)
        ps = psS[0].next()
        p.mm(ps[:, :n], ones[:], sq[:, :n])
        rstd_from_ss(p, rstd[:, :n], ps[:, :n], 1.0 / 128, eps_t[:])
        out = ob.next()
        p.stt("dve", out[:, :n], o[:, :n], gsub[:, 0:1], rstd[:, :n], ALU.mult, ALU.mult)
        p.store(dst, out[:, :n])

    if with_ctx:
        attend(qc, 0, LC, LC // 128, oc_o)
    for qb in range(L // 512):
        attend(qT, qb * 512, 512, NK // 128, o_o[:, qb * 512:(qb + 1) * 512])
    p.finish()
    return nc


def run_AT(li, inp, a, with_ctx):
    nc = build_AT(li, with_ctx)
    maps = []
    for j in range(NCORES):
        b, h = j // 4, j % 4
        hs = slice(h * 128, (h + 1) * 128)
        kall = np.concatenate([a["k_c"][b][:, hs], a["k"][b][:, hs]], axis=0)
        vall = np.concatenate([a["v_c"][b][:, hs], a["v"][b][:, hs]], axis=0)
        m = {
            "qT": np.ascontiguousarray(a["q"][b][:, hs].T), "kT": np.ascontiguousarray(kall.T),
            "v": np.ascontiguousarray(vall.reshape(NK // 128, 128, 128).transpose(1, 0, 2)),
            "lamp": np.ascontiguousarray(inp["lam_p"][li].reshape(1, 256)),
            "subg": np.ascontiguousarray(inp["subln_g"][li].reshape(128, 1)),
        }
        if with_ctx:
            m["qcT"] = np.ascontiguousarray(a["q_c"][b][:, hs].T)
        maps.append(m)
    res = run_bass_kernel_spmd(nc, maps, core_ids=list(range(NCORES))).results
    att = np.stack([np.concatenate([res[b * 4 + h]["oT"] for h in range(4)], axis=0) for b in range(B)])
    attc = None
    if with_ctx:
        attc = np.stack([np.concatenate([res[b * 4 + h]["ocT"] for h in range(4)], axis=0) for b in range(B)])
    return att, attc


NHC = FFH // 128


def build_C(with_ctx):
    nc = bass.Bass("TRN2", target_bir_lowering=False)
    ntok = NTOK if with_ctx else TPC
    xTd = dram_in(nc, "xT", [128, 8, ntok])
    cTd = dram_in(nc, "catT", [128, 8, ntok])
    modv = dram_in(nc, "modv", [2, 128, 6, 8])
    ng = dram_in(nc, "ng", [128, 8])
    woutd = dram_in(nc, "wout", [128, 8, D])
    w1d = dram_in(nc, "w1", [NHC, 2, 128, 8, 128])
    w2d = dram_in(nc, "w2", [8, 128, NHC, 128])
    x_o = dram_out(nc, "xoT", [128, 8, ntok])

    p = P(nc)
    ones = p.sb([128, 128]); p.memset("dve", ones[:], 1.0)
    eps_t = p.sb([128, 1]); p.memset("dve", eps_t[:], EPS)
    wout = p.sb([128, 8, D], name="wout"); p.load(wout[:], woutd)
    m1 = load_mods(p, modv, ng, 0)
    m2 = load_mods(p, modv, ng, 1)
    xrot = Rot([p.sb([128, 8, 512], name="xT") for _ in range(1)])
    crot = Rot([p.sb([128, 8, 512], name="cT") for _ in range(1)])
    x1 = p.sb([128, 8, 512], name="x1T")
    aT = p.sb([128, NHC, 512], name="aT")
    sq_rot = Rot([p.sb([128, 512], name="sq") for _ in range(2)])
    rstd = p.sb([128, 512], name="rstd")
    sil = Rot([p.sb([128, 512], name="sil") for _ in range(2)])
    w1rot = Rot([p.sb([128, 2, 8, 128], name="w1") for _ in range(2)])
    w2rot = Rot([p.sb([128, NHC, 128], name="w2") for _ in range(2)])
    pss = p.ps([128, 512], name="pss")
    pmm = Rot([p.ps([128, 512], name="pmm") for _ in range(3)])
    pg = Rot([p.ps([128, 512], name="pg") for _ in range(2)])
    pu = Rot([p.ps([128, 512], name="pu") for _ in range(2)])

    groups = [(g * 512, 512, 0) for g in range(4)] + ([(TPC, 128, 1)] if with_ctx else [])
    for t0, n, ms in groups:
        xT = xrot.next(); cT = crot.next()
        p.load(xT[:, :, :n], xTd[:, :, t0:t0 + n])
        p.load(cT[:, :, :n], cTd[:, :, t0:t0 + n])
        g1 = m1[ms][1]
        gm2, mt2, o2 = m2[ms]
        for f in range(8):
            pt = pmm.next()
            for k in range(8):
                p.mm(pt[:, :n], wout[:, k, f * 128:(f + 1) * 128], cT[:, k, :n], start=(k == 0), stop=(k == 7))
            p.stt("dve", x1[:, f, :n], pt[:, :n], g1[:, 2, f:f + 1], xT[:, f, :n], ALU.mult, ALU.add)
        hT = cT
        norm_mod(p, hT, x1, n, gm2, mt2, o2, ones, eps_t, pss, sq_rot, rstd)
        for m in range(NHC):
            wt = w1rot.next()
            p.load(wt[:, 0], w1d[m, 0]); p.load(wt[:, 1], w1d[m, 1])
            g_ = pg.next(); u_ = pu.next()
            for k in range(8):
                p.mm(g_[:, :n], wt[:, 0, k, :], hT[:, k, :n], start=(k == 0), stop=(k == 7))
            for k in range(8):
                p.mm(u_[:, :n], wt[:, 1, k, :], hT[:, k, :n], start=(k == 0), stop=(k == 7))
            s_ = sil.next()
            p.act(s_[:, :n], g_[:, :n], AF.Silu)
            p.tt("dve", aT[:, m, :n], u_[:, :n], s_[:, :n], ALU.mult)
        for f in range(8):
            wt = w2rot.next()
            p.load(wt[:], w2d[f])
            pt = pmm.next()
            for m in range(NHC):
                p.mm(pt[:, :n], wt[:, m, :], aT[:, m, :n], start=(m == 0), stop=(m == NHC - 1))
            p.stt("dve", xT[:, f, :n], pt[:, :n], mt2[:, 5, f:f + 1], x1[:, f, :n], ALU.mult, ALU.add)
        p.store(x_o[:, :, t0:t0 + n], xT[:, :, :n])
    p.finish()
    return nc


def run_C(li, inp, mod, x_lat, x_ctx, cat_lat, cat_ctx, with_ctx):
    nc = build_C(with_ctx)
    w1 = inp["ffn_w1"][li].reshape(8, 128, 2, NHC, 128).transpose(3, 2, 1, 0, 4)
    w2 = inp["ffn_w2"][li].reshape(NHC, 128, 8, 128).transpose(2, 1, 0, 3)
    w1 = np.ascontiguousarray(w1); w2 = np.ascontiguousarray(w2)
    wout = np.ascontiguousarray(inp["w_out"][li].reshape(8, 128, D).transpose(1, 0, 2))
    maps = []
    for j in range(NCORES):
        b, r = j // 4, j % 4
        jj = j % 4
        cb, cc = jj // 2, jj % 2
        xs = [x_lat[b, r * TPC:(r + 1) * TPC]]; cs = [cat_lat[b, r * TPC:(r + 1) * TPC]]
        if with_ctx:
            xs.append(x_ctx[cb, cc * 128:(cc + 1) * 128]); cs.append(cat_ctx[cb, cc * 128:(cc + 1) * 128])
        maps.append({"xT": fm(np.concatenate(xs, 0)), "catT": fm(np.concatenate(cs, 0)),
                     "modv": mods_for_core(mod[li], j), "ng": vec8(inp["norm2_g"][li]),
                     "wout": wout, "w1": w1, "w2": w2})
    res = run_bass_kernel_spmd(nc, maps, core_ids=list(range(NCORES))).results

    def unfm(t):
        return t.transpose(2, 1, 0).reshape(t.shape[2], D)
    xl = np.stack([np.concatenate([unfm(res[b * 4 + r]["xoT"][:, :, :TPC]) for r in range(4)], 0) for b in range(B)])
    xc = None
    if with_ctx:
        xc = np.stack([np.concatenate([unfm(res[cb * 2 + cc]["xoT"][:, :, TPC:]) for cc in range(2)], 0) for cb in range(B)])
    return xl, xc


CPC = HYW // NCORES
GS = 4


def fft_consts(N1):
    N = N1 * 128
    f64 = np.float64
    n1 = np.arange(N1, dtype=f64)
    a = 2 * np.pi * np.outer(n1, n1) / N1
    FA = np.concatenate([np.cos(a), -np.sin(a)], axis=1)
    n2 = np.arange(128, dtype=f64)
    tw = 2 * np.pi * np.outer(n2, n1) / N
    Tw = np.stack([np.cos(tw), -np.sin(tw)], axis=1)
    b = 2 * np.pi * np.outer(n2, n2) / 128
    F2 = np.stack([np.cos(b), np.sin(b), -np.sin(b)], axis=1)
    R = np.stack([np.concatenate([np.cos(b), np.sin(b)], 1), np.concatenate([-np.sin(b), np.cos(b)], 1)], axis=1)
    cTw = np.stack([np.cos(tw.T), np.sin(tw.T)], axis=1)
    GA = np.stack([np.cos(a)[:, :N1 // 2], -np.sin(a)[:, :N1 // 2]], axis=1) / N
    return {k: np.ascontiguousarray(v.astype(np.float32)) for k, v in
            dict(FA=FA, Tw=Tw, F2=F2, R=R, cTw=cTw, GA=GA).items()}


class FC:
    pass


def build_HY(with_ctx):
    nc = bass.Bass("TRN2", target_bir_lowering=False)
    cfgs = [("L", 128, L)] + ([("C", 4, LC)] if with_ctx else [])
    dr = {}
    for tag, N1, Lx in cfgs:
        dr[tag] = dict(
            ph=dram_in(nc, "ph" + tag, [3, 3, B, CPC, Lx]),
            kc=dram_in(nc, "kc" + tag, [N1, 2, CPC, 128]),
            FA=dram_in(nc, "FA" + tag, [N1, 2 * N1]), Tw=dram_in(nc, "Tw" + tag, [128, 2, N1]),
            cTw=dram_in(nc, "cTw" + tag, [N1, 2, 128]), GA=dram_in(nc, "GA" + tag, [N1, 2, N1 // 2]),
            y=dram_out(nc, "y" + tag, [B, CPC, Lx]))
    F2d = dram_in(nc, "F2", [128, 3, 128]); Rd = dram_in(nc, "R", [128, 2, 256])
    cwd = dram_in(nc, "cw", [1, 3 * 4 * CPC]); hbd = dram_in(nc, "hbias", [1, 2 * CPC])

    p = P(nc)
    F2 = p.sb([128, 3, 128]); p.load(F2[:], F2d)
    R = p.sb([128, 2, 256]); p.load(R[:], Rd)
    cwt = p.sb([128, 3, 4, CPC]); p.load(cwt[:].rearrange("p a b c -> p (a b c)"), cwd.broadcast_to([128, 3 * 4 * CPC]))
    hbt = p.sb([128, 2, CPC]); p.load(hbt[:].rearrange("p a c -> p (a c)"), hbd.broadcast_to([128, 2 * CPC]))
    kc = p.sb([128, 2, CPC, 128], name="kc")
    PA = p.ps([128, GS, 256], name="PA"); PXr = p.ps([128, 512], name="PXr"); PXi = p.ps([128, 512], name="PXi")
    PD = p.ps([128, GS, 256], name="PD"); PY = p.ps([128, 512], name="PY")
    W = lambda nm, k=2: Rot([p.sb([128, 512], name=nm) for _ in range(k)])
    m1r, m2r, m3r, m4r = W("m1", 1), W("m2", 1), W("m3", 1), W("m4", 1)
    Brr, Bir, Yrr, Yir, Err, Eir = W("Br"), W("Bi"), W("Yr"), W("Yi"), W("Er"), W("Ei")
    shr = [[Rot([p.sb([128, 512], name="sh") for _ in range(1)]) for _ in range(3)] for _ in range(3)]
    accr = [Rot([p.sb([128, 512], name="acc") for _ in range(2)]) for _ in range(3)]
    a2r = W("a2", 3); tr = W("t"); zr = W("z"); yr = W("yo")

    def v3(ap, rows, G, n):
        return ap[0:rows, 0:G * n].rearrange("p (g n) -> p g n", g=G)

    for tag, N1, Lx in cfgs:
        N1h = N1 // 2
        d = dr[tag]
        FA = p.sb([N1, 2 * N1], name="FA"); p.load(FA[:], d["FA"])
        Tw = p.sb([128, 2, N1], name="Tw"); p.load(Tw[:], d["Tw"])
        cTw = p.sb([N1, 2, 128], name="cTw"); p.load(cTw[:], d["cTw"])
        GA = p.sb([N1, 2, N1h], name="GA"); p.load(GA[:], d["GA"])
        Hre = p.sb([128, 2 * CPC, N1], name="Hre"); Him = p.sb([128, 2 * CPC, N1], name="Him")
        p.load(kc[0:N1], d["kc"])
        G = GS
        GN = G * N1

        def fwd(u, ka):
            for g in range(G):
                p.mm(PA[:, g, 0:2 * N1], u[:, g, :], FA[0:ka, :])
            Are, Aim = PA[:, :, 0:N1], PA[:, :, N1:2 * N1]
            Tre = Tw[:, 0, :].unsqueeze(1).broadcast_to([128, G, N1]); Tim = Tw[:, 1, :].unsqueeze(1).broadcast_to([128, G, N1])
            m1, m2, m3, m4 = m1r.next(), m2r.next(), m3r.next(), m4r.next()
            Br, Bi = Brr.next(), Bir.next()
            p.tt("dve", v3(m1, 128, G, N1), Are, Tre, ALU.mult); p.tt("dve", v3(m2, 128, G, N1), Aim, Tim, ALU.mult)
            p.tt("pool", Br[:, :GN], m1[:, :GN], m2[:, :GN], ALU.subtract)
            p.tt("dve", v3(m3, 128, G, N1), Are, Tim, ALU.mult); p.tt("dve", v3(m4, 128, G, N1), Aim, Tre, ALU.mult)
            p.tt("pool", Bi[:, :GN], m3[:, :GN], m4[:, :GN], ALU.add)
            p.mm(PXr[:, :GN], F2[:, 0, :], Br[:, :GN], start=True, stop=False)
            p.mm(PXr[:, :GN], F2[:, 1, :], Bi[:, :GN], start=False, stop=True)
            p.mm(PXi[:, :GN], F2[:, 0, :], Bi[:, :GN], start=True, stop=False)
            p.mm(PXi[:, :GN], F2[:, 2, :], Br[:, :GN], start=False, stop=True)

        def conv(u, h0, obias, gate, out):
            fwd(u, N1h)
            Hr = Hre[:, h0:h0 + G, :].rearrange("p g n -> p (g n)"); Hi = Him[:, h0:h0 + G, :].rearrange("p g n -> p (g n)")
            m1, m2, m3, m4 = m1r.next(), m2r.next(), m3r.next(), m4r.next()
            Yr, Yi = Yrr.next(), Yir.next()
            p.tt("dve", m1[:, :GN], PXr[:, :GN], Hr, ALU.mult); p.tt("dve", m2[:, :GN], PXi[:, :GN], Hi, ALU.mult)
            p.tt("pool", Yr[:, :GN], m1[:, :GN], m2[:, :GN], ALU.subtract)
            p.tt("dve", m3[:, :GN], PXr[:, :GN], Hi, ALU.mult); p.tt("dve", m4[:, :GN], PXi[:, :GN], Hr, ALU.mult)
            p.tt("pool", Yi[:, :GN], m3[:, :GN], m4[:, :GN], ALU.add)
            for g in range(G):
                p.mm(PD[0:N1, g, :], Yr[:, g * N1:(g + 1) * N1], R[:, 0, :], start=True, stop=False)
                p.mm(PD[0:N1, g, :], Yi[:, g * N1:(g + 1) * N1], R[:, 1, :], start=False, stop=True)
            Dre, Dim = PD[0:N1, :, 0:128], PD[0:N1, :, 128:256]
            cr = cTw[:, 0, :].unsqueeze(1).broadcast_to([N1, G, 128]); ci = cTw[:, 1, :].unsqueeze(1).broadcast_to([N1, G, 128])
            m1, m2, m3, m4 = m1r.next(), m2r.next(), m3r.next(), m4r.next()
            Er, Ei = Err.next(), Eir.next()
            p.tt("dve", v3(m1, N1, G, 128), Dre, cr, ALU.mult); p.tt("dve", v3(m2, N1, G, 128), Dim, ci, ALU.mult)
            p.tt("pool", Er[0:N1, :], m1[0:N1, :], m2[0:N1, :], ALU.subtract)
            p.tt("dve", v3(m3, N1, G, 128), Dre, ci, ALU.mult); p.tt("dve", v3(m4, N1, G, 128), Dim, cr, ALU.mult)
            p.tt("pool", Ei[0:N1, :], m3[0:N1, :], m4[0:N1, :], ALU.add)
            p.mm(PY[0:N1h, :], GA[:, 0, :], Er[0:N1, :], start=True, stop=False)
            p.mm(PY[0:N1h, :], GA[:, 1, :], Ei[0:N1, :], start=False, stop=True)
            t = tr.next()
            p.tt("pool", v3(t, N1h, G, 128), u, obias, ALU.mult)
            p.tt("dve", t[0:N1h, :], PY[0:N1h, :], t[0:N1h, :], ALU.add)
            p.tt("dve", out, v3(t, N1h, G, 128), gate, ALU.mult)

        for o in range(2):
            for g in range(CPC // G):
                fwd(kc[0:N1, o, g * G:(g + 1) * G, :], N1)
                h0 = o * CPC + g * G
                p.copy("act", Hre[:, h0:h0 + G, :].rearrange("p g n -> p (g n)"), PXr[:, :GN])
                p.copy("act", Him[:, h0:h0 + G, :].rearrange("p g n -> p (g n)"), PXi[:, :GN])
        for b in range(B):
            for g in range(CPC // G):
                cs = slice(g * G, (g + 1) * G)
                xs = []
                for i in range(3):
                    sh = []
                    for k in range(3):
                        s_ = shr[i][k].next()
                        p.load(v3(s_, N1h, G, 128), d["ph"][i, k, b, cs, :].rearrange("s (n1 n2) -> n1 s n2", n2=128))
                        sh.append(s_)
                    acc = accr[i].next(); a2 = a2r.next()
                    wv = lambda k: cwt[0:N1h, i, k, cs].unsqueeze(2).broadcast_to([N1h, G, 128])
                    e1, e2 = ("pool", "dve") if i % 2 == 0 else ("dve", "pool")
                    p.tt(e1, v3(acc, N1h, G, 128), v3(sh[0], N1h, G, 128), wv(0), ALU.mult)
                    p.tt(e2, v3(a2, N1h, G, 128), v3(sh[1], N1h, G, 128), wv(1), ALU.mult)
                    p.tt(e1, acc[0:N1h, :], acc[0:N1h, :], a2[0:N1h, :], ALU.add)
                    a3 = a2r.next()
                    p.tt(e2, v3(a3, N1h, G, 128), v3(sh[2], N1h, G, 128), wv(2), ALU.mult)
                    p.tt(e1, acc[0:N1h, :], acc[0:N1h, :], a3[0:N1h, :], ALU.add)
                    p.tt(e1, v3(acc, N1h, G, 128), v3(acc, N1h, G, 128), wv(3), ALU.add)
                    xs.append(acc)
                x1, x2, vv = xs
                z = zr.next(); yo = yr.next()
                ob = lambda o: hbt[0:N1h, o, cs].unsqueeze(2).broadcast_to([N1h, G, 128])
                conv(v3(vv, N1h, G, 128), 0 * CPC + g * G, ob(0), v3(x1, N1h, G, 128), v3(z, N1h, G, 128))
                conv(v3(z, N1h, G, 128), 1 * CPC + g * G, ob(1), v3(x2, N1h, G, 128), v3(yo, N1h, G, 128))
                p.store(d["y"][b, cs, :].rearrange("s (n1 n2) -> n1 s n2", n2=128), v3(yo, N1h, G, 128))
    p.finish()
    return nc


def circ_filter(filt, Lx):
    h = filt.reshape(Lx, 2, 2, HYW)
    zero = np.zeros((1, 2, HYW), np.float32)
    return np.concatenate([h[:, :, 0], zero, h[:0:-1, :, 1]], axis=0)


def shifted3(pt):
    B_, Lx, _ = pt.shape
    cm = pt.transpose(0, 2, 1).reshape(B_, 3, HYW, Lx)
    out = np.zeros((3, 3, B_, HYW, Lx), np.float32)
    for i in range(3):
        out[i, 1] = cm[:, i]
        out[i, 0, :, :, 1:] = cm[:, i, :, :-1]
        out[i, 2, :, :, :-1] = cm[:, i, :, 1:]
    return out


def run_HY(li, inp, filtL, filtC, p_hy, p_hy_c, with_ctx):
    nc = build_HY(with_ctx)
    cfgs = [("L", 128, L, filtL, p_hy)] + ([("C", 4, LC, filtC, p_hy_c)] if with_ctx else [])
    pre = {}
    for tag, N1, Lx, filt, pt in cfgs:
        kcirc = circ_filter(filt, Lx).reshape(N1, 128, 2, HYW).transpose(0, 2, 3, 1)
        pre[tag] = (fft_consts(N1), kcirc, shifted3(pt))
    c128 = fft_consts(128)
    cw = np.concatenate([inp["hy_conv_w"][li], inp["hy_conv_b"][li][None]], axis=0)
    cw = cw.reshape(4, 3, HYW).transpose(1, 0, 2)
    maps = []
    for j in range(NCORES):
        cs = slice(j * CPC, (j + 1) * CPC)
        m = {"F2": c128["F2"], "R": c128["R"],
             "cw": np.ascontiguousarray(cw[:, :, cs]).reshape(1, -1),
             "hbias": np.ascontiguousarray(inp["hy_bias"][li][:, cs]).reshape(1, -1)}
        for tag, N1, Lx, filt, pt in cfgs:
            c, kcirc, sh = pre[tag]
            m["ph" + tag] = np.ascontiguousarray(sh[:, :, :, cs, :])
            m["kc" + tag] = np.ascontiguousarray(kcirc[:, :, cs, :])
            for k in ("FA", "Tw", "cTw", "GA"):
                m[k + tag] = c[k]
        maps.append(m)
    res = run_bass_kernel_spmd(nc, maps, core_ids=list(range(NCORES))).results
    hy = np.concatenate([r["yL"] for r in res], axis=1).transpose(0, 2, 1)
    hyc = np.concatenate([r["yC"] for r in res], axis=1).transpose(0, 2, 1) if with_ctx else None
    return np.ascontiguousarray(hy), hyc


def kernel(**inputs):
    inp = {k: np.ascontiguousarray(np.asarray(v, dtype=np.float32)) for k, v in inputs.items()}
    mod, filtL, filtC = run_prep(inp)
    x = inp["x"]
    ctx = inp["ctx"]
    for li in range(DEPTH):
        last = li == DEPTH - 1
        a = run_A(li, inp, mod, x, ctx)
        hy, hyc = run_HY(li, inp, filtL[li], filtC, a["p_hy"], a["p_hy_c"], not last)
        att, attc = run_AT(li, inp, a, not last)
        cat = np.concatenate([hy, a["sg"], att.transpose(0, 2, 1)], axis=-1)
        catc = None
        if not last:
            catc = np.concatenate([hyc, a["sg_c"], attc.transpose(0, 2, 1)], axis=-1)
        x, ctx_new = run_C(li, inp, mod, x, ctx, cat, catc, not last)
        if not last:
            ctx = ctx_new
    return np.ascontiguousarray(x.astype(np.float32))
```

```python
import math
import numpy as np
import concourse.bass as bass
import concourse.mybir as mybir
from concourse.bass_utils import run_bass_kernel_spmd

F32 = mybir.dt.float32
BF16 = mybir.dt.bfloat16
ALU = mybir.AluOpType
AF = mybir.ActivationFunctionType
AX = mybir.AxisListType

NCORES = 8


class T:
    def __init__(self, h, name):
        self.h = h
        self.name = name
        self.w = None
        self.r = []
        self.sem = None
        self.semcnt = 0

    def __getitem__(self, idx):
        return self.h[idx]


class P:
    ENG = ("pe", "act", "dve", "pool", "sp")

    def __init__(self, nc):
        self.nc = nc
        self.q = {e: [] for e in self.ENG}
        self.cnt = {e: 0 for e in self.ENG}
        self.esem = {e: nc.alloc_semaphore("es_" + e) for e in self.ENG}
        self.waited = {}
        self.nt = 0
        self.out_events = []
        self.reg = {}

    def sb(self, shape, dtype=F32, name=None):
        self.nt += 1
        name = "%s_%d" % (name or "t", self.nt)
        t = T(self.nc.alloc_sbuf_tensor(name, list(shape), dtype), name)
        self.reg[name] = t
        return t

    def ps(self, shape, dtype=F32, name=None):
        self.nt += 1
        name = "%s_%d" % (name or "p", self.nt)
        t = T(self.nc.alloc_psum_tensor(name, list(shape), dtype), name)
        self.reg[name] = t
        return t

    def t(self, ap):
        if ap is None or isinstance(ap, (int, float)):
            return None
        return self.reg.get(ap.name)

    def mm(self, out, lhsT, rhs, start=True, stop=True):
        self.op("pe", lambda e: e.matmul(out, lhsT=lhsT, rhs=rhs, start=start, stop=stop),
                reads=[self.t(lhsT), self.t(rhs)], writes=[self.t(out)])

    def act(self, out, in_, func, bias=0.0, scale=1.0, accum_out=None):
        kw = {}
        if accum_out is not None:
            kw["accum_out"] = accum_out
        self.op("act", lambda e: e.activation(out=out, in_=in_, func=func, bias=bias, scale=scale, **kw),
                reads=[self.t(in_), self.t(bias), self.t(scale)], writes=[self.t(out), self.t(accum_out)])

    def tt(self, eng, out, in0, in1, op):
        self.op(eng, lambda e: e.tensor_tensor(out=out, in0=in0, in1=in1, op=op),
                reads=[self.t(in0), self.t(in1)], writes=[self.t(out)])

    def ts(self, eng, out, in0, s1, s2, op0, op1=None):
        if op1 is None:
            f = lambda e: e.tensor_scalar(out=out, in0=in0, scalar1=s1, scalar2=None, op0=op0)
        else:
            f = lambda e: e.tensor_scalar(out=out, in0=in0, scalar1=s1, scalar2=s2, op0=op0, op1=op1)
        self.op(eng, f, reads=[self.t(in0), self.t(s1), self.t(s2)], writes=[self.t(out)])

    def stt(self, eng, out, in0, scalar, in1, op0, op1):
        self.op(eng, lambda e: e.scalar_tensor_tensor(out=out, in0=in0, scalar=scalar, in1=in1, op0=op0, op1=op1),
                reads=[self.t(in0), self.t(scalar), self.t(in1)], writes=[self.t(out)])

    def copy(self, eng, out, in_):
        if eng == "act":
            f = lambda e: e.copy(out=out, in_=in_)
        else:
            f = lambda e: e.tensor_copy(out=out, in_=in_)
        self.op(eng, f, reads=[self.t(in_)], writes=[self.t(out)])

    def recip(self, out, in_):
        self.op("dve", lambda e: e.reciprocal(out=out, in_=in_), reads=[self.t(in_)], writes=[self.t(out)])

    def reduce(self, out, in_, op=ALU.add, axis=AX.X):
        self.op("dve", lambda e: e.tensor_reduce(out=out, in_=in_, axis=axis, op=op),
                reads=[self.t(in_)], writes=[self.t(out)])

    def memset(self, eng, out, val):
        self.op(eng, lambda e: e.memset(out, val), writes=[self.t(out)])

    def load(self, out, src, q="sp"):
        self.dma(lambda e: e.dma_start(out=out, in_=src), reads=[self.t(src)], writes=[self.t(out)], q=q)

    def store(self, dst, in_, q="pool"):
        self.dma(lambda e: e.dma_start(out=dst, in_=in_), reads=[self.t(in_)], writes=[], q=q, is_output=True)

    def _need(self, eng, ev, waits):
        if ev is None:
            return
        sem, val, src = ev
        if src == "pe" and eng == "pe":
            return
        key = (eng, id(sem))
        if self.waited.get(key, 0) >= val:
            return
        self.waited[key] = val
        waits.append((sem, val))

    def _deps(self, eng, reads, writes):
        waits = []
        for t in reads:
            self._need(eng, t.w, waits)
        for t in writes:
            self._need(eng, t.w, waits)
            for ev in t.r:
                self._need(eng, ev, waits)
        return waits

    def op(self, eng, fn, reads=(), writes=()):
        reads = [t for t in reads if isinstance(t, T)]
        writes = [t for t in writes if isinstance(t, T)]
        waits = self._deps(eng, reads, writes)
        self.cnt[eng] += 1
        ev = (self.esem[eng], self.cnt[eng], eng)
        for t in reads:
            if t not in writes:
                t.r.append(ev)
        for t in writes:
            t.w = ev
            t.r = []
        self.q[eng].append((waits, fn, (self.esem[eng], 1)))

    def dma(self, fn, reads=(), writes=(), q="sp", is_output=False):
        reads = [t for t in reads if isinstance(t, T)]
        writes = [t for t in writes if isinstance(t, T)]
        waits = self._deps(q, reads, writes)
        owner = (writes + reads)[0]
        if owner.sem is None:
            owner.sem = self.nc.alloc_semaphore("ds_" + owner.name)
        owner.semcnt += 16
        ev = (owner.sem, owner.semcnt, "dma")
        for t in reads:
            t.r.append(ev)
        for t in writes:
            t.w = ev
            t.r = []
        if is_output:
            self.out_events.append(ev)
        self.q[q].append((waits, fn, (owner.sem, 16)))

    def finish(self):
        fin = []
        for ev in self.out_events:
            self._need("sp", ev, fin)
        self.q["sp"].append((fin, None, None))
        nc = self.nc

        def replay(eng_name):
            def run(e):
                for waits, fn, inc in self.q[eng_name]:
                    for sem, val in waits:
                        e.wait_ge(sem, val)
                    if fn is not None:
                        ins = fn(e)
                        ins.then_inc(inc[0], inc[1])
            return run

        with nc.Block() as block:
            block.tensor(replay("pe"))
            block.scalar(replay("act"))
            block.vector(replay("dve"))
            block.gpsimd(replay("pool"))
            block.sync(replay("sp"))


D = 1024
B = 2
L = 8192
LC = 256
DEPTH = 2
GRID_W = 64
EPS = 1e-6
HYW = 256
HY_IN = 768
SG_IN = 512
DA_W = 512
IN_W = 2816
FFH = 2816
TPC = 2048
NTOK = TPC + 128
MAGIC = 12582912.0
TWO_PI = 2.0 * math.pi


class Rot:
    def __init__(self, tiles):
        self.tiles = tiles
        self.i = 0

    def next(self):
        t = self.tiles[self.i % len(self.tiles)]
        self.i += 1
        return t


def dram_in(nc, name, shape, dtype=F32):
    return nc.dram_tensor(name, list(shape), dtype, kind="ExternalInput").ap()


def dram_out(nc, name, shape, dtype=F32):
    return nc.dram_tensor(name, list(shape), dtype, kind="ExternalOutput").ap()


def sin_reduced(p, out, arg, tmp):
    p.ts("dve", tmp, arg, 1.0 / TWO_PI, MAGIC, ALU.mult, ALU.add)
    p.ts("dve", tmp, tmp, MAGIC, None, ALU.subtract)
    p.stt("dve", tmp, arg, 1.0 / TWO_PI, tmp, ALU.mult, ALU.subtract)
    p.act(out, tmp, AF.Sin, scale=TWO_PI)


def rstd_from_ss(p, out, ss, inv_n, eps_tile):
    p.act(out, ss, AF.Sqrt, scale=inv_n, bias=eps_tile)
    p.recip(out, out)


PREP_NL = L // NCORES


def build_prep():
    nc = bass.Bass("TRN2", target_bir_lowering=False)
    c3T = dram_in(nc, "c3T", [128, 8, 3])
    adaw = dram_in(nc, "adaw", [DEPTH, 128, 8, 768])
    adab = dram_in(nc, "adab", [DEPTH, 1, 768])
    mod_o = dram_out(nc, "mod", [DEPTH, 3, 768])
    jobs = []
    for li in range(DEPTH):
        jobs.append((li, PREP_NL, "L"))
    jobs.append((0, LC, "C"))
    feats = {"L": dram_in(nc, "featsL", [33, PREP_NL]), "C": dram_in(nc, "featsC", [33, LC])}
    decay = {"L": dram_in(nc, "decayL", [PREP_NL, HYW]), "C": dram_in(nc, "decayC", [LC, HYW])}
    w1 = dram_in(nc, "hw1", [DEPTH, 33, 64])
    w2 = dram_in(nc, "hw2", [DEPTH, 64, 64])
    w3 = dram_in(nc, "hw3", [DEPTH, 64, 1024])
    hb = dram_in(nc, "hb", [DEPTH, 64, 4])
    filt_o = {(li, kind): dram_out(nc, "filt_%d_%s" % (li, kind), [n, 1024]) for li, n, kind in jobs}

    p = P(nc)
    ct = p.sb([128, 8, 3]); p.load(ct[:], c3T)
    st = p.sb([128, 8, 3])
    p.act(st[:], ct[:], AF.Silu)
    wrot = Rot([p.sb([128, 8, 384], name="adaw") for _ in range(2)])
    pm = p.ps([128, 512], name="pm")
    for li in range(DEPTH):
        bt = p.sb([3, 768], name="adab")
        p.load(bt[:], adab[li].broadcast_to([3, 768]))
        mt = p.sb([3, 768], name="modo")
        for h in range(2):
            wt = wrot.next()
            p.load(wt[:], adaw[li, :, :, h * 384:(h + 1) * 384])
            for k in range(8):
                p.mm(pm[0:3, 0:384], st[:, k, :], wt[:, k, :], start=(k == 0), stop=(k == 7))
            p.tt("dve", mt[:, h * 384:(h + 1) * 384], pm[0:3, 0:384], bt[:, h * 384:(h + 1) * 384], ALU.add)
        p.store(mod_o[li], mt[:])
    ph = p.ps([128, 512], name="ph")
    po = Rot([p.ps([128, 512], name="po") for _ in range(2)])
    for li, n, kind in jobs:
        w1t = p.sb([33, 64]); p.load(w1t[:], w1[li])
        w2t = p.sb([64, 64]); p.load(w2t[:], w2[li])
        w3t = p.sb([64, 1024]); p.load(w3t[:], w3[li])
        hbt = p.sb([64, 4]); p.load(hbt[:], hb[li])
        ft = p.sb([33, n]); p.load(ft[:], feats[kind])
        h2 = p.sb([64, n], name="h2")
        nb = min(n, 512)
        for j in range(n // nb):
            sl = slice(j * nb, (j + 1) * nb)
            arg = p.sb([64, nb], name="arg"); tmp = p.sb([64, nb], name="tmp"); h1 = p.sb([64, nb], name="h1")
            p.mm(ph[0:64, 0:nb], w1t[:], ft[:, sl])
            p.ts("dve", arg[:], ph[0:64, 0:nb], hbt[:, 0:1], hbt[:, 1:2], ALU.add, ALU.mult)
            sin_reduced(p, h1[:], arg[:], tmp[:])
            p.mm(ph[0:64, 0:nb], w2t[:], h1[:])
            p.ts("dve", arg[:], ph[0:64, 0:nb], hbt[:, 2:3], hbt[:, 3:4], ALU.add, ALU.mult)
            sin_reduced(p, h2[:, sl], arg[:], tmp[:])
        orot = Rot([p.sb([128, 1024], name="fo") for _ in range(2)])
        for j in range(n // 128):
            dt = p.sb([128, HYW], name="dec")
            p.load(dt[:], decay[kind][j * 128:(j + 1) * 128, :])
            ot = orot.next()
            for h in range(2):
                pt = po.next()
                p.mm(pt[:], h2[:, j * 128:(j + 1) * 128], w3t[:, h * 512:(h + 1) * 512])
                p.tt("dve", ot[:, h * 512:(h + 1) * 512].rearrange("p (g c) -> p g c", c=HYW),
                     pt[:].rearrange("p (g c) -> p g c", c=HYW),
                     dt[:].unsqueeze(1).broadcast_to([128, 2, HYW]), ALU.mult)
            p.store(filt_o[(li, kind)][j * 128:(j + 1) * 128, :], ot[:])
    p.finish()
    return nc


def hyena_consts(n):
    t = np.linspace(0.0, 1.0, n, dtype=np.float32).astype(np.float64)[:, None]
    bands = np.linspace(1e-4, 15, 16, dtype=np.float32).astype(np.float64)
    ang = (2.0 * math.pi / n) * np.arange(n, dtype=np.float64)[:, None] * bands[None, :]
    feats = np.concatenate([t, np.cos(ang), -np.sin(ang)], axis=-1)
    min_decay = math.log(1e-2) / 1.5
    max_decay = math.log(1e-2) / 0.3
    deltas = np.abs(np.linspace(min_decay, max_decay, HYW, dtype=np.float32).astype(np.float64))
    dec = np.exp(-t * deltas[None, :])
    return np.ascontiguousarray(feats.T.astype(np.float32)), dec.astype(np.float32)


def run_prep(inp):
    nc = build_prep()
    c3 = np.concatenate([inp["c"], inp["c_ctx"][None, :]], axis=0)
    c3T = np.ascontiguousarray(c3.T.reshape(8, 128, 3).transpose(1, 0, 2))
    featsL, decL = hyena_consts(L)
    featsC, decC = hyena_consts(LC)
    hb = np.stack([inp["hy_b1"], inp["hy_freq"][:, 0], inp["hy_b2"], inp["hy_freq"][:, 1]], axis=-1)
    maps = []
    for j in range(NCORES):
        aw = inp["ada_w"][:, :, j * 768:(j + 1) * 768].reshape(DEPTH, 8, 128, 768).transpose(0, 2, 1, 3)
        maps.append({
            "c3T": c3T, "adaw": np.ascontiguousarray(aw),
            "adab": np.ascontiguousarray(inp["ada_b"][:, None, j * 768:(j + 1) * 768]),
            "featsL": np.ascontiguousarray(featsL[:, j * PREP_NL:(j + 1) * PREP_NL]), "featsC": featsC,
            "decayL": np.ascontiguousarray(decL[j * PREP_NL:(j + 1) * PREP_NL]), "decayC": decC,
            "hw1": inp["hy_w1"], "hw2": inp["hy_w2"], "hw3": inp["hy_w3"], "hb": np.ascontiguousarray(hb),
        })
    res = run_bass_kernel_spmd(nc, maps, core_ids=list(range(NCORES))).results
    mod = np.concatenate([r["mod"] for r in res], axis=-1)
    filtL = np.stack([np.concatenate([r["filt_%d_L" % li] for r in res], axis=0) for li in range(DEPTH)])
    filtC = res[0]["filt_0_C"]
    return mod, filtL, filtC


def load_mods(p, modv, ng, which):
    out = []
    ngt = p.sb([128, 8], name="ng"); p.load(ngt[:], ng)
    for s in range(2):
        mt = p.sb([128, 6, 8], name="modv"); p.load(mt[:], modv[s])
        gm = p.sb([128, 8], name="gm")
        o = 3 * which
        p.ts("dve", gm[:], mt[:, o + 1, :], 1.0, None, ALU.add)
        p.tt("dve", gm[:], gm[:], ngt[:], ALU.mult)
        out.append((gm, mt, o))
    return out


def norm_mod(p, hT, xT, n, gm, mt, o, ones, eps_t, pss, sq_rot, rstd):
    for k in range(8):
        sq = sq_rot.next()
        p.act(sq[:, :n], xT[:, k, :n], AF.Square)
        p.mm(pss[:, :n], ones[:], sq[:, :n], start=(k == 0), stop=(k == 7))
    rstd_from_ss(p, rstd[:, :n], pss[:, :n], 1.0 / D, eps_t[:])
    for k in range(8):
        p.stt("dve", hT[:, k, :n], xT[:, k, :n], gm[:, k:k + 1], rstd[:, :n], ALU.mult, ALU.mult)
        p.act(hT[:, k, :n], hT[:, k, :n], AF.Identity, bias=mt[:, o, k:k + 1])


A_COLS = [(0, 512), (512, 768), (768, 1280), (1280, 1792), (1792, 2304), (2304, 2816)]


def build_A():
    nc = bass.Bass("TRN2", target_bir_lowering=False)
    xTd = dram_in(nc, "xT", [128, 8, NTOK])
    modv = dram_in(nc, "modv", [2, 128, 6, 8])
    ng = dram_in(nc, "ng", [128, 8])
    wind = dram_in(nc, "win", [128, 8, IN_W])
    sgg = dram_in(nc, "sgg", [1, 256])
    wsT = dram_in(nc, "wsT", [128, 4, 128])
    sgb = dram_in(nc, "sgb", [128, 4])
    qkg = dram_in(nc, "qkg", [2, 1, 64])
    ropeC = dram_in(nc, "ropeC", [128, 16, 64])
    ropeS = dram_in(nc, "ropeS", [128, 16, 64])
    hy_o = dram_out(nc, "p_hy", [NTOK, HY_IN])
    sg_o = dram_out(nc, "sg", [NTOK, 256])
    q_o = dram_out(nc, "q", [NTOK, 512], BF16)
    k_o = dram_out(nc, "k", [NTOK, 512], BF16)
    v_o = dram_out(nc, "v", [NTOK, 512], BF16)

    p = P(nc)
    ones = p.sb([128, 128]); p.memset("dve", ones[:], 1.0)
    eps_t = p.sb([128, 1]); p.memset("dve", eps_t[:], EPS)
    win = p.sb([128, 8, IN_W], name="win"); p.load(win[:], wind)
    mods = load_mods(p, modv, ng, 0)
    sggt = p.sb([128, 256]); p.load(sggt[:], sgg.broadcast_to([128, 256]))
    wst = p.sb([128, 4, 128]); p.load(wst[:], wsT)
    sgbt = p.sb([128, 4]); p.load(sgbt[:], sgb)
    gq = p.sb([128, 64]); p.load(gq[:], qkg[0].broadcast_to([128, 64]))
    p.ts("dve", gq[:], gq[:], 0.125, None, ALU.mult)
    gk = p.sb([128, 64]); p.load(gk[:], qkg[1].broadcast_to([128, 64]))
    rc = p.sb([128, 16, 64]); p.load(rc[:], ropeC)
    rs = p.sb([128, 16, 64]); p.load(rs[:], ropeS)

    xrot = Rot([p.sb([128, 8, 512], name="xT") for _ in range(2)])
    hrot = Rot([p.sb([128, 8, 512], name="hT") for _ in range(2)])
    sq_rot = Rot([p.sb([128, 512], name="sq") for _ in range(2)])
    rstd = p.sb([128, 512], name="rstd")
    pss = p.ps([128, 512], name="pss")
    pmm = Rot([p.ps([128, 512], name="pmm") for _ in range(4)])
    psg = p.ps([128, 256], name="psg")
    o_hy = Rot([p.sb([128, HY_IN], name="ohy") for _ in range(2)])
    o_sg = Rot([p.sb([128, 256], name="osg") for _ in range(2)])
    gel = Rot([p.sb([128, 512], name="gel") for _ in range(2)])
    vn = Rot([p.sb([128, 256], name="vn") for _ in range(2)])
    junk = p.sb([128, 256], name="junk")
    s1 = Rot([p.sb([128, 1], name="s1") for _ in range(2)])
    qsq = Rot([p.sb([128, 512], name="qsq") for _ in range(2)])
    qn = Rot([p.sb([128, 512], name="qn") for _ in range(2)])
    t1 = Rot([p.sb([128, 512], name="t1") for _ in range(2)])
    t2 = Rot([p.sb([128, 512], name="t2") for _ in range(2)])
    s8 = Rot([p.sb([128, 8], name="s8") for _ in range(2)])
    o_qk = Rot([p.sb([128, 512], BF16, name="oqk") for _ in range(3)])

    groups = [(g * 512, 512, 0) for g in range(4)] + [(TPC, 128, 1)]
    for t0, n, ms in groups:
        gm, mt, o = mods[ms]
        xT = xrot.next(); hT = hrot.next()
        p.load(xT[:, :, :n], xTd[:, :, t0:t0 + n])
        norm_mod(p, hT, xT, n, gm, mt, o, ones, eps_t, pss, sq_rot, rstd)
        for s in range(n // 128):
            tok = slice(t0 + s * 128, t0 + (s + 1) * 128)
            ti = (t0 + s * 128) // 128
            ohy = o_hy.next()
            for ci, (c0, c1) in enumerate(A_COLS):
                w = c1 - c0
                pt = pmm.next()
                for k in range(8):
                    p.mm(pt[:, :w], hT[:, k, s * 128:(s + 1) * 128], win[:, k, c0:c1], start=(k == 0), stop=(k == 7))
                if ci < 2:
                    p.copy("act", ohy[:, c0:c1], pt[:, :w])
                    if ci == 1:
                        p.store(hy_o[tok, :], ohy[:])
                elif ci == 2:
                    ge = gel.next(); v_n = vn.next(); ss = s1.next(); osg = o_sg.next()
                    p.act(ge[:], pt[:], AF.Gelu)
                    p.act(junk[:], ge[:, 256:512], AF.Square, accum_out=ss[:])
                    rstd_from_ss(p, ss[:], ss[:], 1.0 / 256, eps_t[:])
                    p.stt("dve", v_n[:], ge[:, 256:512], ss[:, 0:1], sggt[:], ALU.mult, ALU.mult)
                    for h in range(4):
                        p.mm(psg[:, h * 64:(h + 1) * 64], wst[:, h, :], v_n[:, h * 64:(h + 1) * 64])
                    for h in range(4):
                        p.stt("dve", osg[:, h * 64:(h + 1) * 64], psg[:, h * 64:(h + 1) * 64], sgbt[:, h:h + 1],
                              ge[:, h * 64:(h + 1) * 64], ALU.add, ALU.mult)
                    p.store(sg_o[tok, :], osg[:])
                elif ci in (3, 4):
                    g_t = gq if ci == 3 else gk
                    sq = qsq.next(); q_n = qn.next(); s_8 = s8.next(); oq = o_qk.next()
                    p.act(sq[:], pt[:], AF.Square)
                    p.reduce(s_8[:], sq[:].rearrange("p (g d) -> p g d", d=64))
                    rstd_from_ss(p, s_8[:], s_8[:], 1.0 / 64, eps_t[:])
                    p.tt("dve", q_n[:].rearrange("p (g d) -> p g d", d=64), pt[:].rearrange("p (g d) -> p g d", d=64),
                         s_8[:].unsqueeze(2).broadcast_to([128, 8, 64]), ALU.mult)
                    if ms == 0:
                        p.tt("dve", q_n[:].rearrange("p (g d) -> p g d", d=64), q_n[:].rearrange("p (g d) -> p g d", d=64),
                             g_t[:].unsqueeze(1).broadcast_to([128, 8, 64]), ALU.mult)
                        a = t1.next(); b2 = t2.next()
                        p.tt("dve", a[:].rearrange("p (g d) -> p g d", d=64), q_n[:].rearrange("p (g d) -> p g d", d=64),
                             rc[:, ti, :].unsqueeze(1).broadcast_to([128, 8, 64]), ALU.mult)
                        qv = q_n[:].rearrange("p (g a h d) -> p g a h d", g=8, a=2, h=2)
                        bv = b2[:].rearrange("p (g a h d) -> p g a h d", g=8, a=2, h=2)
                        sv = rs[:, ti, :].rearrange("p (a h d) -> p a h d", a=2, h=2)
                        for hh in range(2):
                            p.tt("pool", bv[:, :, :, hh, :], qv[:, :, :, 1 - hh, :],
                                 sv[:, :, hh, :].unsqueeze(1).broadcast_to([128, 8, 2, 16]), ALU.mult)
                        p.tt("dve", oq[:], a[:], b2[:], ALU.add)
                    else:
                        p.tt("dve", oq[:].rearrange("p (g d) -> p g d", d=64), q_n[:].rearrange("p (g d) -> p g d", d=64),
                             g_t[:].unsqueeze(1).broadcast_to([128, 8, 64]), ALU.mult)
                    p.store((q_o if ci == 3 else k_o)[tok, :], oq[:])
                else:
                    ov = o_qk.next()
                    p.copy("act", ov[:], pt[:])
                    p.store(v_o[tok, :], ov[:])
    p.finish()
    return nc


def rope_tables():
    half = 16
    inv = 10000.0 ** (-np.arange(half, dtype=np.float64) / half)
    t = np.arange(L)
    row = (t // GRID_W).astype(np.float64)[:, None] * inv[None, :]
    col = (t % GRID_W).astype(np.float64)[:, None] * inv[None, :]
    C = np.concatenate([np.cos(row), np.cos(row), np.cos(col), np.cos(col)], axis=1)
    S = np.concatenate([-np.sin(row), np.sin(row), -np.sin(col), np.sin(col)], axis=1)
    return C.astype(np.float32), S.astype(np.float32)


def fm(a):
    n = a.shape[0]
    return np.ascontiguousarray(a.T.reshape(8, 128, n).transpose(1, 0, 2))


def vec8(v):
    return np.ascontiguousarray(v.reshape(8, 128).T)


def mods_for_core(mod_l, j):
    b = j // 4
    out = np.zeros((2, 128, 6, 8), np.float32)
    for s, r in enumerate((b, 2)):
        m = mod_l[r].reshape(6, 8, 128)
        out[s] = m.transpose(2, 0, 1)
    return out


def core_tokens(xT_lat, xT_ctx, j):
    raise NotImplementedError


def run_A(li, inp, mod, x_lat, x_ctx):
    nc = build_A()
    ropeC, ropeS = rope_tables()
    maps = []
    for j in range(NCORES):
        b, r = j // 4, j % 4
        cb, cc = (r // 2, r % 2)
        if j >= 4:
            cb, cc = ((j - 4) // 2, (j - 4) % 2)
        xt = np.concatenate([x_lat[b, r * TPC:(r + 1) * TPC], x_ctx[cb, cc * 128:(cc + 1) * 128]], axis=0)
        tl = slice(r * TPC, (r + 1) * TPC)
        maps.append({
            "xT": fm(xt), "modv": mods_for_core(mod[li], j) if j < 4 or True else None, "ng": vec8(inp["norm1_g"][li]),
            "win": np.ascontiguousarray(inp["w_in"][li].reshape(8, 128, IN_W).transpose(1, 0, 2)),
            "sgg": np.ascontiguousarray(inp["sg_norm_g"][li][None, :]),
            "wsT": np.ascontiguousarray(inp["sg_w"][li].transpose(2, 0, 1)),
            "sgb": np.ascontiguousarray(inp["sg_b"][li].T),
            "qkg": np.ascontiguousarray(np.stack([inp["qn_g"][li], inp["kn_g"][li]])[:, None, :]),
            "ropeC": np.ascontiguousarray(ropeC[tl].reshape(16, 128, 64).transpose(1, 0, 2)),
            "ropeS": np.ascontiguousarray(ropeS[tl].reshape(16, 128, 64).transpose(1, 0, 2)),
        })
    res = run_bass_kernel_spmd(nc, maps, core_ids=list(range(NCORES))).results
    out = {}
    for name, w in (("p_hy", HY_IN), ("sg", 256), ("q", 512), ("k", 512), ("v", 512)):
        lat = np.stack([np.concatenate([res[b * 4 + r][name][:TPC] for r in range(4)], axis=0) for b in range(B)])
        ctx = np.stack([np.concatenate([res[cb * 2 + cc][name][TPC:] for cc in range(2)], axis=0) for cb in range(B)])
        out[name] = lat
        out[name + "_c"] = ctx
    return out


NK = LC + L


def build_AT(li, with_ctx):
    lam_init = 0.8 - 0.6 * math.exp(-0.3 * li)
    nc = bass.Bass("TRN2", target_bir_lowering=False)
    qTd = dram_in(nc, "qT", [128, L], BF16)
    kTd = dram_in(nc, "kT", [128, NK], BF16)
    vd = dram_in(nc, "v", [128, NK // 128, 128], BF16)
    lamp = dram_in(nc, "lamp", [1, 256])
    subg = dram_in(nc, "subg", [128, 1])
    o_o = dram_out(nc, "oT", [128, L])
    if with_ctx:
        qcd = dram_in(nc, "qcT", [128, LC], BF16)
        oc_o = dram_out(nc, "ocT", [128, LC])

    p = P(nc)
    ones = p.sb([128, 128]); p.memset("dve", ones[:], 1.0)
    onesb = p.sb([128, 128], BF16); p.memset("dve", onesb[:], 1.0)
    eps_t = p.sb([128, 1]); p.memset("dve", eps_t[:], EPS)
    qT = p.sb([128, L], BF16, name="qT"); p.load(qT[:], qTd)
    kT = p.sb([128, NK], BF16, name="kT"); p.load(kT[:], kTd)
    vt = p.sb([128, NK // 128, 128], BF16, name="v"); p.load(vt[:], vd)
    lp = p.sb([128, 256]); p.load(lp[:], lamp.broadcast_to([128, 256]))
    pr = p.sb([128, 128]); d2 = p.sb([128, 2]); nlam = p.sb([128, 1]); gsub = p.sb([128, 1])
    lv = lp[:].rearrange("p (a b d) -> p a b d", a=2, b=2)
    p.tt("dve", pr[:].rearrange("p (a d) -> p a d", a=2), lv[:, :, 0, :], lv[:, :, 1, :], ALU.mult)
    p.reduce(d2[:], pr[:].rearrange("p (a d) -> p a d", a=2))
    p.act(d2[:], d2[:], AF.Exp)
    p.tt("dve", nlam[:], d2[:, 1:2], d2[:, 0:1], ALU.subtract)
    p.ts("dve", nlam[:], nlam[:], -lam_init, None, ALU.add)
    p.load(gsub[:], subg)
    p.ts("dve", gsub[:], gsub[:], 1.0 - lam_init, None, ALU.mult)
    if with_ctx:
        qc = p.sb([128, LC], BF16); p.load(qc[:], qcd)

    psS = [Rot([p.ps([128, 512], name="S%d" % c) for _ in range(2)]) for c in range(2)]
    psAV = [p.ps([128, 512], name="AV%d" % c) for c in range(2)]
    psSUM = [p.ps([128, 512], name="SUM%d" % c) for c in range(2)]
    Er = [Rot([p.sb([128, 512], BF16, name="E%d" % c) for _ in range(3)]) for c in range(2)]
    rr = Rot([p.sb([128, 512], name="rr") for _ in range(2)])
    aa = Rot([p.sb([128, 512], name="aa") for _ in range(2)])
    oo = Rot([p.sb([128, 512], name="oo") for _ in range(2)])
    sq = p.sb([128, 512], name="sq"); rstd = p.sb([128, 512], name="rstd")
    ob = Rot([p.sb([128, 512], name="ob") for _ in range(2)])

    def attend(qsrc, q0, n, nkt, dst):
        def qk(kt):
            tiles = []
            for c in range(2):
                ps = psS[c].next(); E = Er[c].next()
                pc = slice(c * 64, (c + 1) * 64)
                p.mm(ps[:, :n], kT[pc, kt * 128:(kt + 1) * 128], qsrc[pc, q0:q0 + n])
                p.act(E[:, :n], ps[:, :n], AF.Exp)
                tiles.append(E)
            return tiles

        def av(kt, tiles):
            for c in range(2):
                p.mm(psAV[c][:, :n], vt[:, kt, :], tiles[c][:, :n], start=(kt == 0), stop=(kt == nkt - 1))
                p.mm(psSUM[c][:, :n], onesb[:], tiles[c][:, :n], start=(kt == 0), stop=(kt == nkt - 1))

        prev = qk(0)
        for kt in range(1, nkt):
            cur = qk(kt)
            av(kt - 1, prev)
            prev = cur
        av(nkt - 1, prev)
        a_ = []
        for c in range(2):
            r = rr.next(); a = aa.next()
            p.recip(r[:, :n], psSUM[c][:, :n])
            p.tt("dve", a[:, :n], psAV[c][:, :n], r[:, :n], ALU.mult)
            a_.append(a)
        o = oo.next()
        p.stt("dve", o[:, :n], a_[1][:, :n], nlam[:, 0:1], a_[0][:, :n], ALU.mult, ALU.add)
        p.act(sq[:, :n], o[:, :n], AF.Square)
        ps = psS[0].next()
        p.mm(ps[:, :n], ones[:], sq[:, :n])
        rstd_from_ss(p, rstd[:, :n], ps[:, :n], 1.0 / 128, eps_t[:])
        out = ob.next()
        p.stt("dve", out[:, :n], o[:, :n], gsub[:, 0:1], rstd[:, :n], ALU.mult, ALU.mult)
        p.store(dst, out[:, :n])

    if with_ctx:
        attend(qc, 0, LC, LC // 128, oc_o)
    for qb in range(L // 512):
        attend(qT, qb * 512, 512, NK // 128, o_o[:, qb * 512:(qb + 1) * 512])
    p.finish()
    return nc


def run_AT(li, inp, a, with_ctx):
    nc = build_AT(li, with_ctx)
    maps = []
    for j in range(NCORES):
        b, h = j // 4, j % 4
        hs = slice(h * 128, (h + 1) * 128)
        kall = np.concatenate([a["k_c"][b][:, hs], a["k"][b][:, hs]], axis=0)
        vall = np.concatenate([a["v_c"][b][:, hs], a["v"][b][:, hs]], axis=0)
        m = {
            "qT": np.ascontiguousarray(a["q"][b][:, hs].T), "kT": np.ascontiguousarray(kall.T),
            "v": np.ascontiguousarray(vall.reshape(NK // 128, 128, 128).transpose(1, 0, 2)),
            "lamp": np.ascontiguousarray(inp["lam_p"][li].reshape(1, 256)),
            "subg": np.ascontiguousarray(inp["subln_g"][li].reshape(128, 1)),
        }
        if with_ctx:
            m["qcT"] = np.ascontiguousarray(a["q_c"][b][:, hs].T)
        maps.append(m)
    res = run_bass_kernel_spmd(nc, maps, core_ids=list(range(NCORES))).results
    att = np.stack([np.concatenate([res[b * 4 + h]["oT"] for h in range(4)], axis=0) for b in range(B)])
    attc = None
    if with_ctx:
        attc = np.stack([np.concatenate([res[b * 4 + h]["ocT"] for h in range(4)], axis=0) for b in range(B)])
    return att, attc


NHC = FFH // 128


def build_C(with_ctx):
    nc = bass.Bass("TRN2", target_bir_lowering=False)
    ntok = NTOK if with_ctx else TPC
    xTd = dram_in(nc, "xT", [128, 8, ntok])
    cTd = dram_in(nc, "catT", [128, 8, ntok])
    modv = dram_in(nc, "modv", [2, 128, 6, 8])
    ng = dram_in(nc, "ng", [128, 8])
    woutd = dram_in(nc, "wout", [128, 8, D])
    w1d = dram_in(nc, "w1", [NHC, 2, 128, 8, 128])
    w2d = dram_in(nc, "w2", [8, 128, NHC, 128])
    x_o = dram_out(nc, "xoT", [128, 8, ntok])

    p = P(nc)
    ones = p.sb([128, 128]); p.memset("dve", ones[:], 1.0)
    eps_t = p.sb([128, 1]); p.memset("dve", eps_t[:], EPS)
    wout = p.sb([128, 8, D], name="wout"); p.load(wout[:], woutd)
    m1 = load_mods(p, modv, ng, 0)
    m2 = load_mods(p, modv, ng, 1)
    xrot = Rot([p.sb([128, 8, 512], name="xT") for _ in range(1)])
    crot = Rot([p.sb([128, 8, 512], name="cT") for _ in range(1)])
    x1 = p.sb([128, 8, 512], name="x1T")
    aT = p.sb([128, NHC, 512], name="aT")
    sq_rot = Rot([p.sb([128, 512], name="sq") for _ in range(2)])
    rstd = p.sb([128, 512], name="rstd")
    sil = Rot([p.sb([128, 512], name="sil") for _ in range(2)])
    w1rot = Rot([p.sb([128, 2, 8, 128], name="w1") for _ in range(2)])
    w2rot = Rot([p.sb([128, NHC, 128], name="w2") for _ in range(2)])
    pss = p.ps([128, 512], name="pss")
    pmm = Rot([p.ps([128, 512], name="pmm") for _ in range(3)])
    pg = Rot([p.ps([128, 512], name="pg") for _ in range(2)])
    pu = Rot([p.ps([128, 512], name="pu") for _ in range(2)])

    groups = [(g * 512, 512, 0) for g in range(4)] + ([(TPC, 128, 1)] if with_ctx else [])
    for t0, n, ms in groups:
        xT = xrot.next(); cT = crot.next()
        p.load(xT[:, :, :n], xTd[:, :, t0:t0 + n])
        p.load(cT[:, :, :n], cTd[:, :, t0:t0 + n])
        g1 = m1[ms][1]
        gm2, mt2, o2 = m2[ms]
        for f in range(8):
            pt = pmm.next()
            for k in range(8):
                p.mm(pt[:, :n], wout[:, k, f * 128:(f + 1) * 128], cT[:, k, :n], start=(k == 0), stop=(k == 7))
            p.stt("dve", x1[:, f, :n], pt[:, :n], g1[:, 2, f:f + 1], xT[:, f, :n], ALU.mult, ALU.add)
        hT = cT
        norm_mod(p, hT, x1, n, gm2, mt2, o2, ones, eps_t, pss, sq_rot, rstd)
        for m in range(NHC):
            wt = w1rot.next()
            p.load(wt[:, 0], w1d[m, 0]); p.load(wt[:, 1], w1d[m, 1])
            g_ = pg.next(); u_ = pu.next()
            for k in range(8):
                p.mm(g_[:, :n], wt[:, 0, k, :], hT[:, k, :n], start=(k == 0), stop=(k == 7))
            for k in range(8):
                p.mm(u_[:, :n], wt[:, 1, k, :], hT[:, k, :n], start=(k == 0), stop=(k == 7))
            s_ = sil.next()
            p.act(s_[:, :n], g_[:, :n], AF.Silu)
            p.tt("dve", aT[:, m, :n], u_[:, :n], s_[:, :n], ALU.mult)
        for f in range(8):
            wt = w2rot.next()
            p.load(wt[:], w2d[f])
            pt = pmm.next()
            for m in range(NHC):
                p.mm(pt[:, :n], wt[:, m, :], aT[:, m, :n], start=(m == 0), stop=(m == NHC - 1))
            p.stt("dve", xT[:, f, :n], pt[:, :n], mt2[:, 5, f:f + 1], x1[:, f, :n], ALU.mult, ALU.add)
        p.store(x_o[:, :, t0:t0 + n], xT[:, :, :n])
    p.finish()
    return nc


def run_C(li, inp, mod, x_lat, x_ctx, cat_lat, cat_ctx, with_ctx):
    nc = build_C(with_ctx)
    w1 = inp["ffn_w1"][li].reshape(8, 128, 2, NHC, 128).transpose(3, 2, 1, 0, 4)
    w2 = inp["ffn_w2"][li].reshape(NHC, 128, 8, 128).transpose(2, 1, 0, 3)
    w1 = np.ascontiguousarray(w1); w2 = np.ascontiguousarray(w2)
    wout = np.ascontiguousarray(inp["w_out"][li].reshape(8, 128, D).transpose(1, 0, 2))
    maps = []
    for j in range(NCORES):
        b, r = j // 4, j % 4
        jj = j % 4
        cb, cc = jj // 2, jj % 2
        xs = [x_lat[b, r * TPC:(r + 1) * TPC]]; cs = [cat_lat[b, r * TPC:(r + 1) * TPC]]
        if with_ctx:
            xs.append(x_ctx[cb, cc * 128:(cc + 1) * 128]); cs.append(cat_ctx[cb, cc * 128:(cc + 1) * 128])
        maps.append({"xT": fm(np.concatenate(xs, 0)), "catT": fm(np.concatenate(cs, 0)),
                     "modv": mods_for_core(mod[li], j), "ng": vec8(inp["norm2_g"][li]),
                     "wout": wout, "w1": w1, "w2": w2})
    res = run_bass_kernel_spmd(nc, maps, core_ids=list(range(NCORES))).results

    def unfm(t):
        return t.transpose(2, 1, 0).reshape(t.shape[2], D)
    xl = np.stack([np.concatenate([unfm(res[b * 4 + r]["xoT"][:, :, :TPC]) for r in range(4)], 0) for b in range(B)])
    xc = None
    if with_ctx:
        xc = np.stack([np.concatenate([unfm(res[cb * 2 + cc]["xoT"][:, :, TPC:]) for cc in range(2)], 0) for cb in range(B)])
    return xl, xc


CPC = HYW // NCORES
GS = 4


def fft_consts(N1):
    N = N1 * 128
    f64 = np.float64
    n1 = np.arange(N1, dtype=f64)
    a = 2 * np.pi * np.outer(n1, n1) / N1
    FA = np.concatenate([np.cos(a), -np.sin(a)], axis=1)
    n2 = np.arange(128, dtype=f64)
    tw = 2 * np.pi * np.outer(n2, n1) / N
    Tw = np.stack([np.cos(tw), -np.sin(tw)], axis=1)
    b = 2 * np.pi * np.outer(n2, n2) / 128
    F2 = np.stack([np.cos(b), np.sin(b), -np.sin(b)], axis=1)
    R = np.stack([np.concatenate([np.cos(b), np.sin(b)], 1), np.concatenate([-np.sin(b), np.cos(b)], 1)], axis=1)
    cTw = np.stack([np.cos(tw.T), np.sin(tw.T)], axis=1)
    GA = np.stack([np.cos(a)[:, :N1 // 2], -np.sin(a)[:, :N1 // 2]], axis=1) / N
    return {k: np.ascontiguousarray(v.astype(np.float32)) for k, v in
            dict(FA=FA, Tw=Tw, F2=F2, R=R, cTw=cTw, GA=GA).items()}


class FC:
    pass


def build_HY(with_ctx):
    nc = bass.Bass("TRN2", target_bir_lowering=False)
    cfgs = [("L", 128, L)] + ([("C", 4, LC)] if with_ctx else [])
    dr = {}
    for tag, N1, Lx in cfgs:
        dr[tag] = dict(
            ph=dram_in(nc, "ph" + tag, [3, 3, B, CPC, Lx]),
            kc=dram_in(nc, "kc" + tag, [N1, 2, CPC, 128]),
            FA=dram_in(nc, "FA" + tag, [N1, 2 * N1]), Tw=dram_in(nc, "Tw" + tag, [128, 2, N1]),
            cTw=dram_in(nc, "cTw" + tag, [N1, 2, 128]), GA=dram_in(nc, "GA" + tag, [N1, 2, N1 // 2]),
            y=dram_out(nc, "y" + tag, [B, CPC, Lx]))
    F2d = dram_in(nc, "F2", [128, 3, 128]); Rd = dram_in(nc, "R", [128, 2, 256])
    cwd = dram_in(nc, "cw", [1, 3 * 4 * CPC]); hbd = dram_in(nc, "hbias", [1, 2 * CPC])

    p = P(nc)
    F2 = p.sb([128, 3, 128]); p.load(F2[:], F2d)
    R = p.sb([128, 2, 256]); p.load(R[:], Rd)
    cwt = p.sb([128, 3, 4, CPC]); p.load(cwt[:].rearrange("p a b c -> p (a b c)"), cwd.broadcast_to([128, 3 * 4 * CPC]))
    hbt = p.sb([128, 2, CPC]); p.load(hbt[:].rearrange("p a c -> p (a c)"), hbd.broadcast_to([128, 2 * CPC]))
    kc = p.sb([128, 2, CPC, 128], name="kc")
    PA = p.ps([128, GS, 256], name="PA"); PXr = p.ps([128, 512], name="PXr"); PXi = p.ps([128, 512], name="PXi")
    PD = p.ps([128, GS, 256], name="PD"); PY = p.ps([128, 512], name="PY")
    W = lambda nm, k=2: Rot([p.sb([128, 512], name=nm) for _ in range(k)])
    m1r, m2r, m3r, m4r = W("m1", 1), W("m2", 1), W("m3", 1), W("m4", 1)
    Brr, Bir, Yrr, Yir, Err, Eir = W("Br"), W("Bi"), W("Yr"), W("Yi"), W("Er"), W("Ei")
    shr = [[Rot([p.sb([128, 512], name="sh") for _ in range(1)]) for _ in range(3)] for _ in range(3)]
    accr = [Rot([p.sb([128, 512], name="acc") for _ in range(2)]) for _ in range(3)]
    a2r = W("a2", 3); tr = W("t"); zr = W("z"); yr = W("yo")

    def v3(ap, rows, G, n):
        return ap[0:rows, 0:G * n].rearrange("p (g n) -> p g n", g=G)

    for tag, N1, Lx in cfgs:
        N1h = N1 // 2
        d = dr[tag]
        FA = p.sb([N1, 2 * N1], name="FA"); p.load(FA[:], d["FA"])
        Tw = p.sb([128, 2, N1], name="Tw"); p.load(Tw[:], d["Tw"])
        cTw = p.sb([N1, 2, 128], name="cTw"); p.load(cTw[:], d["cTw"])
        GA = p.sb([N1, 2, N1h], name="GA"); p.load(GA[:], d["GA"])
        Hre = p.sb([128, 2 * CPC, N1], name="Hre"); Him = p.sb([128, 2 * CPC, N1], name="Him")
        p.load(kc[0:N1], d["kc"])
        G = GS
        GN = G * N1

        def fwd(u, ka):
            for g in range(G):
                p.mm(PA[:, g, 0:2 * N1], u[:, g, :], FA[0:ka, :])
            Are, Aim = PA[:, :, 0:N1], PA[:, :, N1:2 * N1]
            Tre = Tw[:, 0, :].unsqueeze(1).broadcast_to([128, G, N1]); Tim = Tw[:, 1, :].unsqueeze(1).broadcast_to([128, G, N1])
            m1, m2, m3, m4 = m1r.next(), m2r.next(), m3r.next(), m4r.next()
            Br, Bi = Brr.next(), Bir.next()
            p.tt("dve", v3(m1, 128, G, N1), Are, Tre, ALU.mult); p.tt("dve", v3(m2, 128, G, N1), Aim, Tim, ALU.mult)
            p.tt("pool", Br[:, :GN], m1[:, :GN], m2[:, :GN], ALU.subtract)
            p.tt("dve", v3(m3, 128, G, N1), Are, Tim, ALU.mult); p.tt("dve", v3(m4, 128, G, N1), Aim, Tre, ALU.mult)
            p.tt("pool", Bi[:, :GN], m3[:, :GN], m4[:, :GN], ALU.add)
            p.mm(PXr[:, :GN], F2[:, 0, :], Br[:, :GN], start=True, stop=False)
            p.mm(PXr[:, :GN], F2[:, 1, :], Bi[:, :GN], start=False, stop=True)
            p.mm(PXi[:, :GN], F2[:, 0, :], Bi[:, :GN], start=True, stop=False)
            p.mm(PXi[:, :GN], F2[:, 2, :], Br[:, :GN], start=False, stop=True)

        def conv(u, h0, obias, gate, out):
            fwd(u, N1h)
            Hr = Hre[:, h0:h0 + G, :].rearrange("p g n -> p (g n)"); Hi = Him[:, h0:h0 + G, :].rearrange("p g n -> p (g n)")
            m1, m2, m3, m4 = m1r.next(), m2r.next(), m3r.next(), m4r.next()
            Yr, Yi = Yrr.next(), Yir.next()
            p.tt("dve", m1[:, :GN], PXr[:, :GN], Hr, ALU.mult); p.tt("dve", m2[:, :GN], PXi[:, :GN], Hi, ALU.mult)
            p.tt("pool", Yr[:, :GN], m1[:, :GN], m2[:, :GN], ALU.subtract)
            p.tt("dve", m3[:, :GN], PXr[:, :GN], Hi, ALU.mult); p.tt("dve", m4[:, :GN], PXi[:, :GN], Hr, ALU.mult)
            p.tt("pool", Yi[:, :GN], m3[:, :GN], m4[:, :GN], ALU.add)
            for g in range(G):
                p.mm(PD[0:N1, g, :], Yr[:, g * N1:(g + 1) * N1], R[:, 0, :], start=True, stop=False)
                p.mm(PD[0:N1, g, :], Yi[:, g * N1:(g + 1) * N1], R[:, 1, :], start=False, stop=True)
            Dre, Dim = PD[0:N1, :, 0:128], PD[0:N1, :, 128:256]
            cr = cTw[:, 0, :].unsqueeze(1).broadcast_to([N1, G, 128]); ci = cTw[:, 1, :].unsqueeze(1).broadcast_to([N1, G, 128])
            m1, m2, m3, m4 = m1r.next(), m2r.next(), m3r.next(), m4r.next()
            Er, Ei = Err.next(), Eir.next()
            p.tt("dve", v3(m1, N1, G, 128), Dre, cr, ALU.mult); p.tt("dve", v3(m2, N1, G, 128), Dim, ci, ALU.mult)
            p.tt("pool", Er[0:N1, :], m1[0:N1, :], m2[0:N1, :], ALU.subtract)
            p.tt("dve", v3(m3, N1, G, 128), Dre, ci, ALU.mult); p.tt("dve", v3(m4, N1, G, 128), Dim, cr, ALU.mult)
            p.tt("pool", Ei[0:N1, :], m3[0:N1, :], m4[0:N1, :], ALU.add)
            p.mm(PY[0:N1h, :], GA[:, 0, :], Er[0:N1, :], start=True, stop=False)
            p.mm(PY[0:N1h, :], GA[:, 1, :], Ei[0:N1, :], start=False, stop=True)
            t = tr.next()
            p.tt("pool", v3(t, N1h, G, 128), u, obias, ALU.mult)
            p.tt("dve", t[0:N1h, :], PY[0:N1h, :], t[0:N1h, :], ALU.add)
            p.tt("dve", out, v3(t, N1h, G, 128), gate, ALU.mult)

        for o in range(2):
            for g in range(CPC // G):
                fwd(kc[0:N1, o, g * G:(g + 1) * G, :], N1)
                h0 = o * CPC + g * G
                p.copy("act", Hre[:, h0:h0 + G, :].rearrange("p g n -> p (g n)"), PXr[:, :GN])
                p.copy("act", Him[:, h0:h0 + G, :].rearrange("p g n -> p (g n)"), PXi[:, :GN])
        for b in range(B):
            for g in range(CPC // G):
                cs = slice(g * G, (g + 1) * G)
                xs = []
                for i in range(3):
                    sh = []
                    for k in range(3):
                        s_ = shr[i][k].next()
                        p.load(v3(s_, N1h, G, 128), d["ph"][i, k, b, cs, :].rearrange("s (n1 n2) -> n1 s n2", n2=128))
                        sh.append(s_)
                    acc = accr[i].next(); a2 = a2r.next()
                    wv = lambda k: cwt[0:N1h, i, k, cs].unsqueeze(2).broadcast_to([N1h, G, 128])
                    e1, e2 = ("pool", "dve") if i % 2 == 0 else ("dve", "pool")
                    p.tt(e1, v3(acc, N1h, G, 128), v3(sh[0], N1h, G, 128), wv(0), ALU.mult)
                    p.tt(e2, v3(a2, N1h, G, 128), v3(sh[1], N1h, G, 128), wv(1), ALU.mult)
                    p.tt(e1, acc[0:N1h, :], acc[0:N1h, :], a2[0:N1h, :], ALU.add)
                    a3 = a2r.next()
                    p.tt(e2, v3(a3, N1h, G, 128), v3(sh[2], N1h, G, 128), wv(2), ALU.mult)
                    p.tt(e1, acc[0:N1h, :], acc[0:N1h, :], a3[0:N1h, :], ALU.add)
                    p.tt(e1, v3(acc, N1h, G, 128), v3(acc, N1h, G, 128), wv(3), ALU.add)
                    xs.append(acc)
                x1, x2, vv = xs
                z = zr.next(); yo = yr.next()
                ob = lambda o: hbt[0:N1h, o, cs].unsqueeze(2).broadcast_to([N1h, G, 128])
                conv(v3(vv, N1h, G, 128), 0 * CPC + g * G, ob(0), v3(x1, N1h, G, 128), v3(z, N1h, G, 128))
                conv(v3(z, N1h, G, 128), 1 * CPC + g * G, ob(1), v3(x2, N1h, G, 128), v3(yo, N1h, G, 128))
                p.store(d["y"][b, cs, :].rearrange("s (n1 n2) -> n1 s n2", n2=128), v3(yo, N1h, G, 128))
    p.finish()
    return nc


def circ_filter(filt, Lx):
    h = filt.reshape(Lx, 2, 2, HYW)
    zero = np.zeros((1, 2, HYW), np.float32)
    return np.concatenate([h[:, :, 0], zero, h[:0:-1, :, 1]], axis=0)


def shifted3(pt):
    B_, Lx, _ = pt.shape
    cm = pt.transpose(0, 2, 1).reshape(B_, 3, HYW, Lx)
    out = np.zeros((3, 3, B_, HYW, Lx), np.float32)
    for i in range(3):
        out[i, 1] = cm[:, i]
        out[i, 0, :, :, 1:] = cm[:, i, :, :-1]
        out[i, 2, :, :, :-1] = cm[:, i, :, 1:]
    return out


def run_HY(li, inp, filtL, filtC, p_hy, p_hy_c, with_ctx):
    nc = build_HY(with_ctx)
    cfgs = [("L", 128, L, filtL, p_hy)] + ([("C", 4, LC, filtC, p_hy_c)] if with_ctx else [])
    pre = {}
    for tag, N1, Lx, filt, pt in cfgs:
        kcirc = circ_filter(filt, Lx).reshape(N1, 128, 2, HYW).transpose(0, 2, 3, 1)
        pre[tag] = (fft_consts(N1), kcirc, shifted3(pt))
    c128 = fft_consts(128)
    cw = np.concatenate([inp["hy_conv_w"][li], inp["hy_conv_b"][li][None]], axis=0)
    cw = cw.reshape(4, 3, HYW).transpose(1, 0, 2)
    maps = []
    for j in range(NCORES):
        cs = slice(j * CPC, (j + 1) * CPC)
        m = {"F2": c128["F2"], "R": c128["R"],
             "cw": np.ascontiguousarray(cw[:, :, cs]).reshape(1, -1),
             "hbias": np.ascontiguousarray(inp["hy_bias"][li][:, cs]).reshape(1, -1)}
        for tag, N1, Lx, filt, pt in cfgs:
            c, kcirc, sh = pre[tag]
            m["ph" + tag] = np.ascontiguousarray(sh[:, :, :, cs, :])
            m["kc" + tag] = np.ascontiguousarray(kcirc[:, :, cs, :])
            for k in ("FA", "Tw", "cTw", "GA"):
                m[k + tag] = c[k]
        maps.append(m)
    res = run_bass_kernel_spmd(nc, maps, core_ids=list(range(NCORES))).results
    hy = np.concatenate([r["yL"] for r in res], axis=1).transpose(0, 2, 1)
    hyc = np.concatenate([r["yC"] for r in res], axis=1).transpose(0, 2, 1) if with_ctx else None
    return np.ascontiguousarray(hy), hyc


def kernel(**inputs):
    inp = {k: np.ascontiguousarray(np.asarray(v, dtype=np.float32)) for k, v in inputs.items()}
    mod, filtL, filtC = run_prep(inp)
    x = inp["x"]
    ctx = inp["ctx"]
    for li in range(DEPTH):
        last = li == DEPTH - 1
        a = run_A(li, inp, mod, x, ctx)
        hy, hyc = run_HY(li, inp, filtL[li], filtC, a["p_hy"], a["p_hy_c"], not last)
        att, attc = run_AT(li, inp, a, not last)
        cat = np.concatenate([hy, a["sg"], att.transpose(0, 2, 1)], axis=-1)
        catc = None
        if not last:
            catc = np.concatenate([hyc, a["sg_c"], attc.transpose(0, 2, 1)], axis=-1)
        x, ctx_new = run_C(li, inp, mod, x, ctx, cat, catc, not last)
        if not last:
            ctx = ctx_new
    return np.ascontiguousarray(x.astype(np.float32))
```
